# Optimizing a Trainium2 kernel written in Bass

```python
import jax, jax.numpy as jnp
from jax import lax
import numpy as np

D_MODEL = 1024
BATCH = 32
SEQ = 2048
DEPTH = 2
DEC_BATCH = 2
DEC_SEQ = 8192
PAST_LEN = 128

N_HEADS = 8
QK_NOPE_DIM = 64
QK_ROPE_DIM = 64
V_HEAD_DIM = 64
Q_LORA_RANK = 384
KV_LORA_RANK = 256
ATTN_WIDTH = N_HEADS * V_HEAD_DIM
ROPE_BASE = 10000.0
Q_BLOCK = 128
CONV_WIDTH = 512
CONV_KERNEL = 31
D_FF = 4 * D_MODEL
NORM_EPS = 1e-6
N_MOD = 6

COL_SPLITS = (
    2 * CONV_WIDTH,
    2 * CONV_WIDTH + Q_LORA_RANK,
    2 * CONV_WIDTH + Q_LORA_RANK + KV_LORA_RANK,
    2 * CONV_WIDTH + Q_LORA_RANK + KV_LORA_RANK + QK_ROPE_DIM,
    2 * CONV_WIDTH + Q_LORA_RANK + KV_LORA_RANK + QK_ROPE_DIM + D_MODEL,
)
IN_COLS = 2 * CONV_WIDTH + Q_LORA_RANK + KV_LORA_RANK + QK_ROPE_DIM + 2 * D_MODEL

kernel_name = "hybrid_conformer_mla_encoder"


def rms_norm(x, g):
    x32 = x.astype(jnp.float32)
    y = x32 * lax.rsqrt(jnp.mean(x32 * x32, axis=-1, keepdims=True) + NORM_EPS)
    return (y * g.astype(jnp.float32)).astype(x.dtype)


def layer_norm(x, g, b):
    x32 = x.astype(jnp.float32)
    mu = jnp.mean(x32, axis=-1, keepdims=True)
    xc = x32 - mu
    y = xc * lax.rsqrt(jnp.mean(xc * xc, axis=-1, keepdims=True) + NORM_EPS)
    return (y * g.astype(jnp.float32) + b.astype(jnp.float32)).astype(x.dtype)


def rope_tables(seq_len):
    inv = ROPE_BASE ** (-jnp.arange(0, QK_ROPE_DIM, 2, dtype=jnp.float32) / QK_ROPE_DIM)
    ang = jnp.arange(seq_len, dtype=jnp.float32)[:, None] * inv[None, :]
    return jnp.cos(ang), jnp.sin(ang)


def apply_rope(x, cos, sin):
    cos = cos.astype(x.dtype)
    sin = sin.astype(x.dtype)
    x1, x2 = jnp.split(x, 2, axis=-1)
    return jnp.concatenate([x1 * cos - x2 * sin, x2 * cos + x1 * sin], axis=-1)


def mla_attention(q_nope, q_rope, k_nope, k_rope, v):
    B, S, H, _ = q_nope.shape
    nb = S // Q_BLOCK
    scale = (QK_NOPE_DIM + QK_ROPE_DIM) ** -0.5
    qn = q_nope.reshape(B, nb, Q_BLOCK, H, QK_NOPE_DIM).swapaxes(0, 1)
    qr = q_rope.reshape(B, nb, Q_BLOCK, H, QK_ROPE_DIM).swapaxes(0, 1)

    def block(args):
        qn_b, qr_b = args
        s = jnp.einsum('bqhd,bkhd->bhqk', qn_b, k_nope, preferred_element_type=jnp.float32)
        s = s + jnp.einsum('bqhr,bkr->bhqk', qr_b, k_rope, preferred_element_type=jnp.float32)
        p = jax.nn.softmax(s * scale, axis=-1).astype(v.dtype)
        return jnp.einsum('bhqk,bkhd->bqhd', p, v)

    o = lax.map(block, (qn, qr))
    return o.swapaxes(0, 1).reshape(B, S, H * V_HEAD_DIM)


def encoder_layer(x, c, cos, sin, ada_w, ada_b, norm_mix_g, w_in, q_norm_g, w_q_up,
                  kv_norm_g, w_kv_up, w_attn_o, conv_dw, conv_dw_b, conv_ln_g, conv_ln_b,
                  w_conv_out, w_out, norm_mlp_g, w_mlp_up, w_mlp_down):
    B, S, _ = x.shape
    mod = jax.nn.silu(c) @ ada_w + ada_b
    shift1, scale1, gate1, shift2, scale2, gate2 = jnp.split(mod[:, None, :], N_MOD, axis=-1)

    h = rms_norm(x, norm_mix_g) * (1 + scale1) + shift1
    proj = h @ w_in
    conv_in, q_a, kv_a, k_rope_raw, g_conv, g_attn = jnp.split(proj, COL_SPLITS, axis=-1)

    a, b = jnp.split(conv_in, 2, axis=-1)
    u = a * jax.nn.sigmoid(b)
    z = lax.conv_general_dilated(
        u, conv_dw.reshape(CONV_KERNEL, 1, CONV_WIDTH),
        window_strides=(1,), padding=[(CONV_KERNEL // 2, CONV_KERNEL // 2)],
        dimension_numbers=('NWC', 'WIO', 'NWC'), feature_group_count=CONV_WIDTH) + conv_dw_b
    z = jax.nn.silu(layer_norm(z, conv_ln_g, conv_ln_b))
    y_conv = z @ w_conv_out

    q = (rms_norm(q_a, q_norm_g) @ w_q_up).reshape(B, S, N_HEADS, QK_NOPE_DIM + QK_ROPE_DIM)
    q_nope, q_rope = jnp.split(q, [QK_NOPE_DIM], axis=-1)
    q_rope = apply_rope(q_rope, cos[:, None, :], sin[:, None, :])
    kv = (rms_norm(kv_a, kv_norm_g) @ w_kv_up).reshape(B, S, N_HEADS, QK_NOPE_DIM + V_HEAD_DIM)
    k_nope, v = jnp.split(kv, [QK_NOPE_DIM], axis=-1)
    k_rope = apply_rope(k_rope_raw, cos, sin)
    y_attn = mla_attention(q_nope, q_rope, k_nope, k_rope, v) @ w_attn_o

    mix = (jax.nn.sigmoid(g_conv) * y_conv + jax.nn.sigmoid(g_attn) * y_attn) @ w_out
    x = x + gate1 * mix

    h2 = rms_norm(x, norm_mlp_g) * (1 + scale2) + shift2
    x = x + gate2 * (jnp.square(jax.nn.relu(h2 @ w_mlp_up)) @ w_mlp_down)
    return x


def trunk(x, c, weights, final_g):
    cos, sin = rope_tables(x.shape[1])
    for l in range(DEPTH):
        x = encoder_layer(x, c, cos, sin, *[w[l] for w in weights])
    return rms_norm(x, final_g)


def setup_inputs(seed: int = 0) -> dict:
    key = jax.random.key(seed)
    ks = jax.random.split(key, 24)
    f32 = jnp.float32

    def w(k, shape, fan_in, s=1.0):
        return jax.random.normal(k, shape, f32) * (s * fan_in ** -0.5)

    def gain(k, shape):
        return 1.0 + 0.01 * jax.random.normal(k, shape, f32)

    def bias(k, shape):
        return 0.01 * jax.random.normal(k, shape, f32)

    L, D = DEPTH, D_MODEL
    return {
        "x_prompt": jax.random.normal(ks[0], (BATCH, SEQ, D), f32),
        "x_sample": jax.random.normal(ks[1], (DEC_BATCH, DEC_SEQ, D), f32),
        "c_prompt": jax.random.normal(ks[2], (BATCH, D), f32),
        "c_sample": jax.random.normal(ks[3], (DEC_BATCH, D), f32),
        "ada_w": w(ks[4], (L, D, N_MOD * D), D, 0.2),
        "ada_b": bias(ks[5], (L, N_MOD * D)),
        "norm_mix_g": gain(ks[6], (L, D)),
        "w_in": w(ks[7], (L, D, IN_COLS), D),
        "q_norm_g": gain(ks[8], (L, Q_LORA_RANK)),
        "w_q_up": w(ks[9], (L, Q_LORA_RANK, N_HEADS * (QK_NOPE_DIM + QK_ROPE_DIM)), Q_LORA_RANK),
        "kv_norm_g": gain(ks[10], (L, KV_LORA_RANK)),
        "w_kv_up": w(ks[11], (L, KV_LORA_RANK, N_HEADS * (QK_NOPE_DIM + V_HEAD_DIM)), KV_LORA_RANK),
        "w_attn_o": w(ks[12], (L, ATTN_WIDTH, D), ATTN_WIDTH),
        "conv_dw": w(ks[13], (L, CONV_KERNEL, CONV_WIDTH), CONV_KERNEL),
        "conv_dw_b": bias(ks[14], (L, CONV_WIDTH)),
        "conv_ln_g": gain(ks[15], (L, CONV_WIDTH)),
        "conv_ln_b": bias(ks[16], (L, CONV_WIDTH)),
        "w_conv_out": w(ks[17], (L, CONV_WIDTH, D), CONV_WIDTH),
        "w_out": w(ks[18], (L, D, D), D),
        "norm_mlp_g": gain(ks[19], (L, D)),
        "w_mlp_up": w(ks[20], (L, D, D_FF), D),
        "w_mlp_down": w(ks[21], (L, D_FF, D), D_FF),
        "final_g": gain(ks[22], (D,)),
    }


def reference(x_prompt, x_sample, c_prompt, c_sample, ada_w, ada_b, norm_mix_g, w_in,
              q_norm_g, w_q_up, kv_norm_g, w_kv_up, w_attn_o, conv_dw, conv_dw_b,
              conv_ln_g, conv_ln_b, w_conv_out, w_out, norm_mlp_g, w_mlp_up, w_mlp_down,
              final_g):
    weights = (ada_w, ada_b, norm_mix_g, w_in, q_norm_g, w_q_up, kv_norm_g, w_kv_up,
               w_attn_o, conv_dw, conv_dw_b, conv_ln_g, conv_ln_b, w_conv_out, w_out,
               norm_mlp_g, w_mlp_up, w_mlp_down)
    y_prompt = trunk(x_prompt, c_prompt, weights, final_g)
    y_sample = trunk(x_sample, c_sample, weights, final_g)
    return (y_prompt, y_sample)
```

```python
import numpy as np
from contextlib import ExitStack

import concourse.bass as bass
import concourse.mybir as mybir
from concourse.bass_utils import run_bass_kernel_spmd

F32 = mybir.dt.float32
BF16 = mybir.dt.bfloat16
AF = mybir.ActivationFunctionType
ALU = mybir.AluOpType

D = 1024
NBK = 512
NH = 8
R = 4
NCORES = 8
EPS = 1e-6
ATT_SCALE = float(128 ** -0.5)

_VOFF = {}
_o = 0
for _name, _n in [("ada_b", 96), ("g1", 16), ("g2", 16), ("fg", 8), ("gq", 6), ("gkv", 4),
                  ("cb", 8), ("lg", 8), ("lb", 8), ("cw", 248), ("mL", R), ("mR", R)]:
    _VOFF[_name] = _o
    _o += _n
NV = _o


class Tok:
    __slots__ = ("sem", "val", "ds")

    def __init__(self, sem, val, ds=None):
        self.sem = sem
        self.val = val
        self.ds = ds


class Eng:
    def __init__(self, nc, eng, name, es):
        self.nc = nc
        self.e = eng
        self.name = name
        self.sem = es.enter_context(nc.semaphore("s_" + name))
        self.count = 0
        self.waited = {}

    def wait(self, tok):
        if tok is None:
            return
        k = id(tok.sem)
        val = tok.val if tok.ds is None else tok.ds.count
        if self.waited.get(k, 0) >= val:
            return
        self.e.wait_ge(tok.sem, val)
        self.waited[k] = val

    def sig(self, ins):
        self.count += 1
        ins.then_inc(self.sem, 1)
        return Tok(self.sem, self.count)


class DSem:
    def __init__(self, nc, name, es, shared=False):
        self.sem = es.enter_context(nc.semaphore("d_" + name))
        self.count = 0
        self.shared = shared

    def sig(self, ins):
        self.count += 16
        ins.then_inc(self.sem, 16)
        return Tok(self.sem, self.count, self if self.shared else None)


class Tracker:
    def __init__(self):
        self.w = {}
        self.r = {}

    def deps(self, reads, writes):
        out = []
        for k in reads:
            t = self.w.get(k)
            if t is not None:
                out.append(t)
        for k in writes:
            t = self.w.get(k)
            if t is not None:
                out.append(t)
            out.extend(self.r.get(k, ()))
        return out

    def record(self, tok, reads, writes):
        for k in reads:
            self.r.setdefault(k, []).append(tok)
        for k in writes:
            self.w[k] = tok
            self.r[k] = []


class Prog:
    def __init__(self, T, NU):
        self.T = T
        self.NU = NU
        self.NBLK = T // NBK
        self.KP = min(T, 2048)
        self.HR = 16384 // T
        self.GR = 2048 + self.HR
        self.nc = bass.Bass("TRN2", target_bir_lowering=False)
        self.tr = Tracker()
        nb = self.NBLK
        self.gi_keys = ([("GI", "K", b) for b in range(nb)] + [("GI", "R", b, h) for b in range(nb) for h in range(8)]
                        + [("GI", "V", b, t) for b in range(nb) for t in range(4)] + [("GI", "H")])

    def op(self, eng, fn, reads=(), writes=()):
        for t in self.tr.deps(reads, writes):
            eng.wait(t)
        tok = eng.sig(fn(eng.e))
        self.tr.record(tok, reads, writes)
        return tok

    def dma(self, eng, dsem, out, in_, reads=(), writes=(), extra=()):
        for t in self.tr.deps(reads, writes):
            eng.wait(t)
        for t in extra:
            eng.wait(t)
        tok = dsem.sig(eng.e.dma_start(out=out, in_=in_))
        self.tr.record(tok, reads, writes)
        return tok

    def mm(self, out_ap, pairs, reads, bank, extra=()):
        pe = self.pe
        writes = (("PS", bank),)
        for t in self.tr.deps(reads, writes):
            pe.wait(t)
        for t in extra:
            pe.wait(t)
        n = len(pairs)
        ins = None
        for i, (l, r) in enumerate(pairs):
            ins = pe.e.matmul(out_ap, lhsT=l, rhs=r, start=(i == 0), stop=(i == n - 1))
        tok = pe.sig(ins)
        self.tr.record(tok, reads, writes)
        return tok

    def mm_step(self, out_ap, lhsT, rhs, first, last, reads, bank):
        pe = self.pe
        wk = (("PS", bank),)
        for t in self.tr.deps(reads, wk if first else ()):
            pe.wait(t)
        tok = pe.sig(pe.e.matmul(out_ap, lhsT=lhsT, rhs=rhs, start=first, stop=last))
        if first:
            self.tr.record(tok, reads, wk)
        else:
            self.tr.record(tok, reads, ())
            self.tr.w[("PS", bank)] = tok
        return tok

    def bank_alloc(self, lo=0):
        for _ in range(16):
            b = self.bank_rr
            self.bank_rr = (self.bank_rr + 1) % 8
            if b >= lo and b not in self.bank_held:
                self.bank_held.add(b)
                return b
        raise RuntimeError("no free PSUM bank")

    def pair_alloc(self):
        for _ in range(4):
            b = 2 * self.pair_rr
            self.pair_rr = (self.pair_rr + 1) % 2
            if b not in self.bank_held and (b + 1) not in self.bank_held:
                self.bank_held.add(b)
                self.bank_held.add(b + 1)
                return b
        for b in (4, 6):
            if b not in self.bank_held and (b + 1) not in self.bank_held:
                self.bank_held.add(b)
                self.bank_held.add(b + 1)
                return b
        raise RuntimeError("no free PSUM bank pair")

    def bank_free(self, b):
        self.bank_held.discard(b)

    def ftile(self):
        i = self.f_rr
        self.f_rr = (self.f_rr + 1) % self.NF
        return i

    def btile(self):
        i = self.b_rr
        self.b_rr = (self.b_rr + 1) % self.NBT
        return i

    def wslot(self):
        i = self.w_rr
        self.w_rr = (self.w_rr + 1) % 3
        return i

    def rtile(self):
        i = self.rt_rr
        self.rt_rr ^= 1
        return i

    def vec(self, name, col):
        o = _VOFF[name] + col
        return self.vecs[:, o:o + 1]

    def build(self):
        nc = self.nc
        T, NU, NBLK = self.T, self.NU, self.NBLK
        GR = self.GR
        with ExitStack() as es:
            self.es = es
            dt = nc.dram_tensor
            self.xs = dt("xs", [NU, T, D], F32, kind="ExternalInput").ap()
            self.ys = dt("ys", [NU, T, D], F32, kind="ExternalOutput").ap()
            self.cT_d = dt("cT", [128, 8, NU], F32, kind="ExternalInput").ap()
            self.vecs_d = dt("vecs", [128, NV], F32, kind="ExternalInput").ap()
            self.ropeC_d = dt("ropeC", [2, 128, T], F32, kind="ExternalInput").ap()
            self.ropeS_d = dt("ropeS", [2, 128, T], F32, kind="ExternalInput").ap()
            self.ada_w = dt("ada_w", [2, D, 6 * D], F32, kind="ExternalInput").ap()
            self.w_in = dt("w_in", [2, D, 3776], F32, kind="ExternalInput").ap()
            self.w_q_up = dt("w_q_up", [2, 384, 1024], F32, kind="ExternalInput").ap()
            self.w_kv_up = dt("w_kv_up", [2, 256, 1024], F32, kind="ExternalInput").ap()
            self.w_attn_o = dt("w_attn_o", [2, 512, 1024], F32, kind="ExternalInput").ap()
            self.w_conv_out = dt("w_conv_out", [2, 512, 1024], F32, kind="ExternalInput").ap()
            self.w_out = dt("w_out", [2, D, D], F32, kind="ExternalInput").ap()
            self.w_up = dt("w_mlp_up", [2, D, 4 * D], F32, kind="ExternalInput").ap()
            self.w_dn = dt("w_mlp_down", [2, 4 * D, D], F32, kind="ExternalInput").ap()
            self.WPRE = [dt(f"WPRE{l}", [128, 8, 1536], BF16).ap() for l in range(2)]
            self.WKV = [dt(f"WKV{l}", [128, 2, 1024], BF16).ap() for l in range(2)]
            self.WQA = [dt(f"WQA{l}", [128, 8, 384], BF16).ap() for l in range(2)]
            self.WQ = [dt(f"WQ{l}", [128, 3, 1024], BF16).ap() for l in range(2)]
            self.WQR = [dt(f"WQR{l}", [128, 3, 1024], BF16).ap() for l in range(2)]
            self.WMG = [dt(f"WMG{l}", [8, 128, 24, 128], BF16).ap() for l in range(2)]
            self.WOUT = [dt(f"WOUT{l}", [128, 8, 1024], BF16).ap() for l in range(2)]
            self.WUP = [dt(f"WUP{l}", [128, 8, 4096], BF16).ap() for l in range(2)]
            self.WDN = [dt(f"WDN{l}", [8, 128, 32, 128], BF16).ap() for l in range(2)]
            self.WDG = [dt(f"WDG{l}", [4, 128, 3968], BF16).ap() for l in range(2)]
            self.GI_t = dt("GI", [GR, T], BF16)
            self.GO_t = dt("GO", [R * GR, T], BF16)
            self.GI = self.GI_t.ap()
            self.GO = self.GO_t.ap()

            sb = lambda name, shape, dtp: es.enter_context(nc.sbuf_tensor(name, shape, dtp))
            self.XT = sb("XT", [128, 8, T], F32)
            self.U = sb("U", [128, 4, T + 32], BF16)
            self.Wb = sb("Wb", [128, 3, 4096], BF16)
            self.DG = sb("DG", [128, 4096], BF16)
            self.identb = sb("identb", [128, 128], BF16)
            self.RT = sb("RT", [128, 2, NBK], F32)
            self.rt_rr = 0
            self.H = sb("H", [128, 8, NBK], BF16)
            self.QA = sb("QA", [128, 3, NBK], BF16)
            self.A = sb("A", [128, 4, NBK], BF16)
            self.ZS = sb("ZS", [128, 4, NBK], BF16)
            self.PT = sb("PT", [128, 4, NBK], BF16)
            self.R32 = sb("R32", [128, 16384], BF16)
            self.NF = 8
            self.NBT = 4
            self.Ft = sb("Ft", [128, self.NF, NBK], F32)
            self.Bt = sb("Bt", [128, self.NBT, NBK], BF16)
            self.RC = sb("RC", [128, 2, NBK], F32)
            self.RS = sb("RS", [128, 2, NBK], F32)
            self.ident = sb("ident", [128, 128], F32)
            self.ones = sb("ones", [128, 128], BF16)
            self.zeros = sb("zeros", [128, 512], BF16)
            self.epsb = sb("epsb", [128, 1], F32)
            self.vecs = sb("vecs_sb", [128, NV], F32)
            self.cTs = sb("cTs", [128, 8, NU], F32)
            self.scT = sb("scT", [128, 8, NU], BF16)
            self.mod = sb("mod", [128, 2, 48, NU], F32)
            self.A1 = sb("A1", [128, 2, NU, 8], F32)
            self.A2 = sb("A2", [128, 2, NU, 8], F32)
            self.Hs = sb("Hs", [128, 4, 32], BF16)
            self.HG = sb("HG", [128, R, 4, 32], BF16)
            R32 = self.R32
            self.Kb = [R32[:, 0:2048], R32[:, 4096:6144]]
            self.Vb = [R32[:, 2048:4096].rearrange("p (k e) -> p k e", e=128),
                       R32[:, 6144:8192].rearrange("p (k e) -> p k e", e=128)]
            self.QT = R32[:, 8192:12288].rearrange("p (h n) -> p h n", n=NBK)
            self.MIX = R32[:, 12288:16384].rearrange("p (h n) -> p h n", n=NBK)
            self.HID = R32[:, :].rearrange("p (h n) -> p h n", n=NBK)
            self.KVA = R32[:, 0:1024].rearrange("p (c n) -> p c n", n=NBK)
            self.KRt = R32[:, 1024:1536]
            self.KTs = self.QT
            self.Vs = R32[:, 12288:16384].rearrange("p (t h e) -> p t h e", t=4, h=8)
            self.PSall = es.enter_context(nc.psum_tensor("psall", [128, 8, NBK], F32))
            self.PS = [self.PSall[:, i, :] for i in range(8)]
            self.pair_rr = 0
            self.ptp_rr = 0
            self.bank_rr = 0
            self.bank_held = set()
            self.f_rr = 0
            self.b_rr = 0
            self.w_rr = 0
            self.pe = Eng(nc, nc.tensor, "pe", es)
            self.act = Eng(nc, nc.scalar, "act", es)
            self.dve = Eng(nc, nc.vector, "dve", es)
            self.pool = Eng(nc, nc.gpsimd, "pool", es)
            self.sp = Eng(nc, nc.sync, "sp", es)
            ds = lambda name: DSem(nc, name, es)
            self.d_w = [ds(f"w{i}") for i in range(4)]
            self.d_kv = [ds(f"kv{i}") for i in range(2)]
            self.d_vv = [ds(f"vv{i}") for i in range(2)]
            self.d_rope = [ds(f"rope{i}") for i in range(2)]
            self.d_ropes = [ds(f"ropes{i}") for i in range(2)]
            self.d_c = ds("cload")
            self.d_wqa = ds("wqa")
            self.d_ada = [ds(f"ada{i}") for i in range(3)]
            self.d_x = [ds(f"x{i}") for i in range(2)]
            self.d_misc = ds("misc")
            self.d_pre = [[ds(f"pc{l}{g}") for g in range(5)] for l in range(2)]
            self.d_dg = ds("dg")
            self.d_dgl = ds("dgl")
            self.d_st = DSem(nc, "store", es, shared=True)
            self.d_out = [ds(f"o{i}") for i in range(2)]
            self.cc_sem = [es.enter_context(nc.semaphore(f"cc{i}")) for i in range(9)]
            self.cc_count = [0] * 9

            import os as _os
            self.stop = int(_os.environ.get("KSTOP", "0"))
            self.startup()
            for u in range(NU):
                if self.stop == 1:
                    break
                self.unit(u)
                if self.stop:
                    break
            for i in range(2):
                if self.d_out[i].count:
                    self.sp.wait(Tok(self.d_out[i].sem, self.d_out[i].count))
        return nc

    def startup(self):
        nc = self.nc
        NU = self.NU
        pool, act, dve, pe, sp = self.pool, self.act, self.dve, self.pe, self.sp
        self.op(pool, lambda e: e.memset(self.ident[:], 0.0), writes=["ident"])
        self.op(pool, lambda e: e.affine_select(out=self.ident[:], in_=self.ident[:], pattern=[[-1, 128]],
                                                compare_op=ALU.not_equal, fill=1.0, base=0,
                                                channel_multiplier=1), reads=["ident"], writes=["ident"])
        self.op(pool, lambda e: e.tensor_copy(out=self.identb[:], in_=self.ident[:]), reads=["ident"],
                writes=["identb"])
        self.op(pool, lambda e: e.memset(self.ones[:], 1.0), writes=["ones"])
        self.op(pool, lambda e: e.memset(self.zeros[:], 0.0), writes=["zeros"])
        self.op(pool, lambda e: e.memset(self.epsb[:], EPS), writes=["epsb"])
        self.dma(sp, self.d_misc, self.vecs[:], self.vecs_d[:, :], writes=["vecs"])
        self.dma(sp, self.d_c, self.cTs[:], self.cT_d[:, :, :], writes=["cTs"])
        self.op(act, lambda e: e.activation(out=self.scT[:], in_=self.cTs[:], func=AF.Silu),
                reads=["cTs"], writes=["scT"])
        for l in range(2):
            for c in range(4):
                for k in range(31):
                    self.op(dve, lambda e: e.tensor_scalar(
                        out=self.DG[:, k * 128:(k + 1) * 128], in0=self.identb[:],
                        scalar1=self.vec("cw", (l * 4 + c) * 31 + k), scalar2=None, op0=ALU.mult),
                        reads=["identb", "vecs"], writes=["DG"])
                self.dma(sp, self.d_dg, self.WDG[l][c, :, :], self.DG[:, 0:3968], reads=["DG"])
        self.dg_tok = Tok(self.d_dg.sem, self.d_dg.count)
        self.pre_tok = [[None] * 5, [None] * 5]
        self.precast(0, 0)
        self.ada(0)
        for g in range(1, 5):
            self.precast(0, g)
        self.deferred = [lambda: self.ada(1), lambda: self.precast(1, 0), lambda: self.precast(1, 1),
                         lambda: self.precast(1, 2), lambda: self.precast(1, 3), lambda: self.precast(1, 4)]

    def run_deferred(self, n):
        for _ in range(n):
            if self.deferred:
                self.deferred.pop(0)()

    def ada(self, l):
        NU = self.NU
        pool, dve = self.pool, self.dve
        if True:
            bank = self.bank_alloc()
            ps = self.PS[bank]
            for piece in range(12):
                s = self.wslot()
                wv = self.Wb[:, s, :].rearrange("p (k n) -> p k n", k=8)
                src = self.ada_w[l, :, piece * 512:(piece + 1) * 512].rearrange("(k p) n -> p k n", p=128)
                self.dma(pool, self.d_ada[s], wv, src, writes=[("W", s)])
                for j in range(4):
                    col = (piece * 4 + j) * 8
                    pairs = [(wv[:, k, j * 128:(j + 1) * 128], self.scT[:, k, :]) for k in range(8)]
                    self.mm(ps[:, col:col + NU], pairs, reads=[("W", s), "scT"], bank=bank)
            psv = ps[:, 0:384].rearrange("p (j u) -> p j u", u=8)
            for u in range(NU):
                self.op(dve, lambda e, u=u: e.tensor_tensor(
                    out=self.mod[:, l, :, u], in0=psv[:, :, u],
                    in1=self.vecs[:, _VOFF["ada_b"] + l * 48:_VOFF["ada_b"] + (l + 1) * 48], op=ALU.add),
                    reads=[("PS", bank), "vecs"], writes=[("mod", l)])
            self.bank_free(bank)
            for u in range(NU):
                self.op(dve, lambda e, u=u: e.scalar_tensor_tensor(
                    out=self.A1[:, l, u, :], in0=self.mod[:, l, 8:16, u], scalar=1.0,
                    in1=self.vecs[:, _VOFF["g1"] + l * 8:_VOFF["g1"] + (l + 1) * 8],
                    op0=ALU.add, op1=ALU.mult), reads=[("mod", l), "vecs"], writes=[("A1", l)])
                self.op(dve, lambda e, u=u: e.scalar_tensor_tensor(
                    out=self.A2[:, l, u, :], in0=self.mod[:, l, 32:40, u], scalar=1.0,
                    in1=self.vecs[:, _VOFF["g2"] + l * 8:_VOFF["g2"] + (l + 1) * 8],
                    op0=ALU.add, op1=ALU.mult), reads=[("mod", l), "vecs"], writes=[("A2", l)])

    def precast(self, l, grp):
        pool = self.pool
        cast = lambda out, in_, reads=(): self.dma(pool, self.d_pre[l][grp], out, in_, reads=reads)
        getattr(self, "_precast%d" % grp)(l, cast)
        self.pre_tok[l][grp] = Tok(self.d_pre[l][grp].sem, self.d_pre[l][grp].count)

    def _precast0(self, l, cast):
        kp = lambda ap: ap.rearrange("(k p) n -> p k n", p=128)
        w_in = self.w_in
        z = self.zeros
        WPRE = self.WPRE[l]
        cast(WPRE[:, :, 0:1024], kp(w_in[l, :, 0:1024]))
        cast(WPRE[:, :, 1024:1280], kp(w_in[l, :, 1408:1664]))
        for k in range(8):
            cast(WPRE[:, k, 1280:1344], z[:, 0:64], reads=["zeros"])
            cast(WPRE[:, k, 1408:1472], z[:, 0:64], reads=["zeros"])
        cast(WPRE[:, :, 1344:1408], kp(w_in[l, :, 1664:1728]))
        cast(WPRE[:, :, 1472:1504], kp(w_in[l, :, 1696:1728]))
        cast(WPRE[:, :, 1504:1536], kp(w_in[l, :, 1664:1696]))
        wkv_src = self.w_kv_up[l].rearrange("(k p) (h e) -> p k h e", p=128, e=128)
        for k in range(2):
            cast(self.WKV[l][:, k, 0:512].rearrange("p (h e) -> p h e", e=64), wkv_src[:, k, :, 0:64])
            cast(self.WKV[l][:, k, 512:1024].rearrange("p (h e) -> p h e", e=64), wkv_src[:, k, :, 64:128])

    def _precast1(self, l, cast):
        kp = lambda ap: ap.rearrange("(k p) n -> p k n", p=128)
        w_in = self.w_in
        z = self.zeros
        cast(self.WQA[l][:, :, :], kp(w_in[l, :, 1024:1408]))
        cast(self.WQ[l][:, :, :], kp(self.w_q_up[l]))
        wq_src = self.w_q_up[l].rearrange("(k p) (h e) -> p k h e", p=128, e=128)
        for k in range(3):
            dst = self.WQR[l][:, k, :].rearrange("p (h e) -> p h e", e=128)
            cast(dst[:, :, 0:64], z[:, 0:512].rearrange("p (h e) -> p h e", e=64), reads=["zeros"])
            cast(dst[:, :, 64:96], wq_src[:, k, :, 96:128])
            cast(dst[:, :, 96:128], wq_src[:, k, :, 64:96])
        for c in range(8):
            cs = slice(c * 128, (c + 1) * 128)
            cast(self.WMG[l][c, :, 0:4, :], kp(self.w_conv_out[l, :, cs]))
            cast(self.WMG[l][c, :, 4:8, :], kp(self.w_attn_o[l, :, cs]))
            cast(self.WMG[l][c, :, 8:16, :], kp(w_in[l, :, 1728 + c * 128:1728 + (c + 1) * 128]))
            cast(self.WMG[l][c, :, 16:24, :], kp(w_in[l, :, 2752 + c * 128:2752 + (c + 1) * 128]))

    def _precast2(self, l, cast):
        kp = lambda ap: ap.rearrange("(k p) n -> p k n", p=128)
        cast(self.WOUT[l][:, :, :], kp(self.w_out[l]))

    def _precast3(self, l, cast):
        kp = lambda ap: ap.rearrange("(k p) n -> p k n", p=128)
        for g in range(4):
            cast(self.WUP[l][:, :, g * 1024:(g + 1) * 1024], kp(self.w_up[l, :, g * 1024:(g + 1) * 1024]))

    def _precast4(self, l, cast):
        kp = lambda ap: ap.rearrange("(k p) n -> p k n", p=128)
        for c in range(8):
            cs = slice(c * 128, (c + 1) * 128)
            cast(self.WDN[l][c, :, :, :], kp(self.w_dn[l, :, cs]))

    def wload(self, src, shape_str=None, grp=1, **kw):
        s = self.wslot()
        n = 1
        for d_ in src.shape[1:]:
            n *= d_
        dst = self.Wb[:, s, 0:n]
        if shape_str is not None:
            dst = dst.rearrange(shape_str, **kw)
        self.dma(self.sp, self.d_w[s], dst, src, writes=[("W", s)], extra=[self.pre_tok[self.l][grp]])
        return s, dst

    def rope_load(self, u, blk):
        i = self.rope_rr
        self.rope_rr ^= 1
        ti = 0 if u == 0 else 1
        cs = slice(blk * NBK, (blk + 1) * NBK)
        self.dma(self.sp, self.d_rope[i], self.RC[:, i, :], self.ropeC_d[ti, :, cs], writes=[("RC", i)])
        self.dma(self.sp, self.d_ropes[i], self.RS[:, i, :], self.ropeS_d[ti, :, cs], writes=[("RS", i)])
        return i

    def stat_begin(self):
        return self.bank_alloc()

    def stat_sq(self, bank, src_ap, src_keys, i, n):
        b = self.btile()
        self.op(self.act, lambda e: e.activation(out=self.Bt[:, b, :], in_=src_ap, func=AF.Square),
                reads=src_keys, writes=[("B", b)])
        self.mm_step(self.PS[bank][:, :], self.ones[:], self.Bt[:, b, :], i == 0, i == n - 1,
                     ["ones", ("B", b)], bank)

    def stat_rstd(self, bank, n):
        ps = self.PS[bank]
        f = self.rtile()
        ft = self.RT[:, f, :]
        self.op(self.act, lambda e: e.activation(out=ft, in_=ps[:, :], func=AF.Ln, scale=1.0 / n,
                                                 bias=self.epsb[:, 0:1]),
                reads=[("PS", bank), "epsb"], writes=[("RT", f)])
        self.bank_free(bank)
        self.op(self.act, lambda e: e.activation(out=ft, in_=ft, func=AF.Exp, scale=-0.5),
                reads=[("RT", f)], writes=[("RT", f)])
        return f

    def main_norm(self, blk, a_ap, b_ap):
        cs = slice(blk * NBK, (blk + 1) * NBK)
        bank = self.stat_begin()
        for c in range(8):
            self.stat_sq(bank, self.XT[:, c, cs], [("XT", c, blk)], c, 8)
        f = self.stat_rstd(bank, D)
        for c in range(8):
            t = self.ftile()
            self.op(self.dve, lambda e: e.tensor_tensor(out=self.Ft[:, t, :], in0=self.XT[:, c, cs],
                                                        in1=self.RT[:, f, :], op=ALU.mult),
                    reads=[("XT", c, blk), ("RT", f)], writes=[("F", t)])
            self.op(self.act, lambda e: e.activation(
                out=self.H[:, c, :], in_=self.Ft[:, t, :], func=AF.Identity, scale=a_ap[:, c:c + 1],
                bias=b_ap[:, c:c + 1]), reads=[("F", t), ("A1", self.l), ("A2", self.l), ("mod", self.l)], writes=[("H", c)])

    def unit(self, u):
        self.u = u
        self.rope_rr = 0
        if u == 0:
            for blk in range(self.NBLK):
                self.load_x_block(u, blk)
        if self.stop == 2:
            return
        for l in range(2):
            self.l = l
            self.pre_pass(u, l)
            if self.stop == 3:
                return
            for blk in range(self.NBLK):
                self.main_block(u, l, blk)
                if u == 0 and l == 0:
                    self.run_deferred(3 if blk == 0 else 2)
                if self.stop == 4:
                    return
            if u == 0 and l == 0:
                self.run_deferred(len(self.deferred))
            if self.stop == 5:
                break
        for blk in range(self.NBLK):
            self.final_out_block(u, blk)
            if u + 1 < self.NU and not self.stop:
                self.load_x_block(u + 1, blk)

    def stageq(self, i):
        return self.R32[:, 8192 + i * 2048:8192 + (i + 1) * 2048].bitcast(F32)

    def stage32(self, i):
        return self.R32[:, 12288 + i * 2048:12288 + (i + 1) * 2048].bitcast(F32)

    def load_x_block(self, u, blk0):
        for tt in range(blk0 * 4, blk0 * 4 + 4):
            i = tt % 2
            st = self.stageq(i)
            keys = [("QT", 4 * i + j) for j in range(4)]
            self.dma(self.sp, self.d_x[i], st, self.xs[u, tt * 128:(tt + 1) * 128, :], writes=keys)
            for half in range(2):
                bank = self.bank_alloc()
                ps = self.PS[bank]
                pe = self.pe
                for t in self.tr.deps(keys + ["ident"], [("PS", bank)]):
                    pe.wait(t)
                ins = None
                for j in range(4):
                    c = half * 4 + j
                    ins = pe.e.transpose(out=ps[:, j * 128:(j + 1) * 128], in_=st[:, c * 128:(c + 1) * 128],
                                         identity=self.ident[:])
                tok = pe.sig(ins)
                self.tr.record(tok, keys + ["ident"], [("PS", bank)])
                eng = self.act if half == 0 else self.dve
                blk = tt // 4
                wk = [("XT", half * 4 + j, blk) for j in range(4)]
                src = ps[:, :].rearrange("p (j n) -> p j n", n=128)
                dst = self.XT[:, half * 4:(half + 1) * 4, tt * 128:(tt + 1) * 128]
                if eng is self.act:
                    self.op(eng, lambda e: e.activation(out=dst, in_=src, func=AF.Copy),
                            reads=[("PS", bank)], writes=wk)
                else:
                    self.op(eng, lambda e: e.tensor_copy(out=dst, in_=src), reads=[("PS", bank)], writes=wk)
                self.bank_free(bank)

    def pre_pass(self, u, l):
        T, NBLK = self.T, self.NBLK
        act, dve, pool, sp = self.act, self.dve, self.pool, self.sp
        sample = (u == 0)
        allw = [("W", i) for i in range(3)] + [("KV", 3)]
        wpre = self.Wb[:, 0:3, :].rearrange("p s n -> p (s n)").rearrange("p (k n) -> p k n", k=8)
        wkv = self.R32[:, 6144:8192].rearrange("p (k n) -> p k n", k=2)
        self.dma(sp, self.d_w[0], wpre, self.WPRE[l][:, :, :], writes=allw, extra=[self.pre_tok[l][0]])
        self.dma(sp, self.d_w[0], wkv, self.WKV[l][:, :, :], writes=allw)
        self.w_rr = 0
        self.conv_build(l, 0)
        self.dg_prefetched = True
        self.wqa_pre = self.R32[:, 2048:5120].rearrange("p (k n) -> p k n", k=8)
        self.dma(sp, self.d_wqa, self.wqa_pre, self.WQA[l][:, :, :], writes=[("KV", 1), ("KV", 2)],
                 extra=[self.pre_tok[l][1]])
        self.wqa_prefetched = True
        self.op(pool, lambda e: e.memset(self.R32[:, 12288:16384], 1.0), writes=[("MIX", j) for j in range(8)])
        a_ap = self.A1[:, l, u, :]
        b_ap = self.mod[:, l, 0:8, u]
        GIv = self.GI[0:2048, :].rearrange("(h q p) t -> p h q t", q=2, p=128)
        for blk in range(NBLK):
            cs = slice(blk * NBK, (blk + 1) * NBK)
            ri = self.rope_load(u, blk)
            if blk == 0:
                self.main_norm(blk, a_ap, b_ap)
            hk = [("H", k) for k in range(8)]

            def glu_chunk(c):
                ba = self.bank_alloc()
                self.mm(self.PS[ba][:, :], [(wpre[:, k, c * 128:(c + 1) * 128], self.H[:, k, :]) for k in range(8)],
                        reads=allw + hk, bank=ba)
                bb = self.bank_alloc()
                self.mm(self.PS[bb][:, :], [(wpre[:, k, 512 + c * 128:512 + (c + 1) * 128], self.H[:, k, :])
                                            for k in range(8)], reads=allw + hk, bank=bb)
                f = self.ftile()
                self.op(act, lambda e: e.activation(out=self.Ft[:, f, :], in_=self.PS[bb][:, :], func=AF.Sigmoid),
                        reads=[("PS", bb)], writes=[("F", f)])
                self.bank_free(bb)
                self.op(dve, lambda e: e.tensor_tensor(
                    out=self.U[:, c, 16 + blk * NBK:16 + (blk + 1) * NBK], in0=self.PS[ba][:, :],
                    in1=self.Ft[:, f, :], op=ALU.mult), reads=[("PS", ba), ("F", f)], writes=[("U", c, blk)])
                self.bank_free(ba)

            kvf = []
            sbank = self.stat_begin()
            for c in range(2):
                bk = self.bank_alloc()
                self.mm(self.PS[bk][:, :], [(wpre[:, k, 1024 + c * 128:1024 + (c + 1) * 128], self.H[:, k, :])
                                            for k in range(8)], reads=allw + hk, bank=bk)
                f = self.ftile()
                self.op(act, lambda e: e.activation(out=self.Ft[:, f, :], in_=self.PS[bk][:, :], func=AF.Copy),
                        reads=[("PS", bk)], writes=[("F", f)])
                self.stat_sq(sbank, self.PS[bk][:, :], [("PS", bk)], c, 2)
                self.bank_free(bk)
                kvf.append(f)
            glu_chunk(0)
            glu_chunk(1)
            fr = self.stat_rstd(sbank, 256)
            for c in range(2):
                self.op(dve, lambda e: e.scalar_tensor_tensor(
                    out=self.KVA[:, c, :], in0=self.Ft[:, kvf[c], :], scalar=self.vec("gkv", l * 2 + c),
                    in1=self.RT[:, fr, :], op0=ALU.mult, op1=ALU.mult),
                    reads=[("F", kvf[c]), ("RT", fr), "vecs"], writes=[("KVA", c)])
            glu_chunk(2)
            glu_chunk(3)
            b1 = self.bank_alloc()
            self.mm(self.PS[b1][:, :], [(wpre[:, k, 1280:1408], self.H[:, k, :]) for k in range(8)],
                    reads=allw + hk, bank=b1)
            b2 = self.bank_alloc()
            self.mm(self.PS[b2][:, :], [(wpre[:, k, 1408:1536], self.H[:, k, :]) for k in range(8)],
                    reads=allw + hk, bank=b2)
            f1 = self.ftile()
            f2 = self.ftile()
            self.op(dve, lambda e: e.tensor_tensor(out=self.Ft[:, f1, :], in0=self.PS[b1][:, :],
                                                   in1=self.RC[:, ri, :], op=ALU.mult),
                    reads=[("PS", b1), ("RC", ri)], writes=[("F", f1)])
            self.bank_free(b1)
            self.op(dve, lambda e: e.tensor_tensor(out=self.Ft[:, f2, :], in0=self.PS[b2][:, :],
                                                   in1=self.RS[:, ri, :], op=ALU.mult),
                    reads=[("PS", b2), ("RS", ri)], writes=[("F", f2)])
            self.bank_free(b2)
            self.op(pool, lambda e: e.tensor_tensor(out=self.KRt, in0=self.Ft[:, f1, :], in1=self.Ft[:, f2, :],
                                                    op=ALU.add), reads=[("F", f1), ("F", f2)], writes=["KRt"])
            if blk + 1 < NBLK:
                self.main_norm(blk + 1, a_ap, b_ap)
            else:
                self.main_norm(0, a_ap, b_ap)
                self.norm_done = True
            for j in range(4):
                bk = self.bank_alloc()
                self.mm(self.PS[bk][:, :], [(wkv[:, k, j * 128:(j + 1) * 128], self.KVA[:, k, :]) for k in range(2)],
                        reads=allw + [("KVA", 0), ("KVA", 1)], bank=bk)
                self.op(act, lambda e, j=j, bk=bk: e.activation(out=self.KTs[0:64, 2 * j, :],
                                                               in_=self.PS[bk][0:64, :], func=AF.Copy),
                        reads=[("PS", bk)], writes=[("QT", 2 * j)])
                self.op(dve, lambda e, j=j, bk=bk: e.tensor_copy(out=self.KTs[0:64, 2 * j + 1, :],
                                                                in_=self.PS[bk][64:128, :]),
                        reads=[("PS", bk)], writes=[("QT", 2 * j + 1)])
                self.bank_free(bk)
            for tt in range(4):
                bk = self.bank_alloc()
                self.mm(self.PS[bk][:, :], [(self.KVA[:, k, tt * 128:(tt + 1) * 128], wkv[:, k, 512:1024])
                                            for k in range(2)], reads=allw + [("KVA", 0), ("KVA", 1)], bank=bk)
                src = self.PS[bk][:, :].rearrange("p (h e) -> p h e", e=64)
                mk = [("MIX", 2 * tt), ("MIX", 2 * tt + 1)]
                if tt % 2 == 0:
                    self.op(act, lambda e, tt=tt, src=src: e.activation(out=self.Vs[:, tt, :, 0:64], in_=src,
                                                                       func=AF.Copy),
                            reads=[("PS", bk)], writes=mk)
                else:
                    self.op(dve, lambda e, tt=tt, src=src: e.tensor_copy(out=self.Vs[:, tt, :, 0:64], in_=src),
                            reads=[("PS", bk)], writes=mk)
                self.bank_free(bk)
            import os as _os
            kpre = int(_os.environ.get("KPRE", "0"))
            if kpre == 1:
                continue
            qk = [("QT", h) for h in range(8)]
            self.dma(sp, self.d_st, GIv[0:64, :, 0, cs], self.KTs[0:64, :, :], reads=qk, writes=[("GI", "K", blk)])
            for h in range(8):
                self.dma(sp, self.d_st, GIv[64:128, h, 0, cs], self.KRt[64:128, :], reads=["KRt"],
                         writes=[("GI", "R", blk, h)])
            for tt in range(4):
                self.dma(sp, self.d_st, GIv[:, :, 1, (blk * 4 + tt) * 128:(blk * 4 + tt + 1) * 128],
                         self.Vs[:, tt, :, :], reads=[("MIX", 2 * tt), ("MIX", 2 * tt + 1)],
                         writes=[("GI", "V", blk, tt)])
        uall = [("U", c, b) for c in range(4) for b in range(NBLK)]
        if kpre in (1, 2):
            return
        if sample:
            self.op(pool, lambda e: e.tensor_copy(out=self.Hs[:, :, 0:16], in_=self.U[:, :, 16:32]),
                    reads=uall, writes=["Hs"])
            self.op(pool, lambda e: e.tensor_copy(out=self.Hs[:, :, 16:32], in_=self.U[:, :, T:T + 16]),
                    reads=uall, writes=["Hs"])
            halo_v = lambda ap: ap.rearrange("r (q x) -> (r q) x", x=128)
            HR = self.HR
            self.dma(sp, self.d_st, halo_v(self.GI[2048:2048 + HR, :]),
                     self.Hs[:, :, :].rearrange("p c j -> p (c j)"), reads=["Hs"], writes=[("GI", "H")])
            pool.wait(Tok(self.d_st.sem, self.d_st.count))
            rg = [[0, 1, 2, 3], [4, 5, 6, 7]]
            for h in ["H"] + list(range(8)):
                if h == "H":
                    src = self.GI[2048:2048 + HR, :]
                    dst = self.GO[8192:8192 + R * HR, :]
                else:
                    src = self.GI[h * 256:(h + 1) * 256, :]
                    dst = self.GO[h * 1024:(h + 1) * 1024, :]
                for t in self.tr.deps(self.gi_keys, [("GO", h)]):
                    pool.wait(t)
                ins = pool.e.collective_compute("AllGather", ALU.bypass, replica_groups=rg, ins=[src], outs=[dst])
                ci = 8 if h == "H" else h
                self.cc_count[ci] += 1
                ins.then_inc(self.cc_sem[ci], 1)
                self.tr.record(Tok(self.cc_sem[ci], self.cc_count[ci]), self.gi_keys, [("GO", h)])
            if kpre == 3:
                return
            for r in range(R):
                base = 8192 + r * HR
                self.dma(sp, self.d_misc, self.HG[:, r, :, :].rearrange("p c j -> p (c j)"),
                         halo_v(self.GO[base:base + HR, :]), reads=[("GO", "H")], writes=["HG"])
            for side, (dst, src_sl, mname) in enumerate([(self.U[:, :, 0:16], slice(16, 32), "mL"),
                                                          (self.U[:, :, T + 16:T + 32], slice(0, 16), "mR")]):
                wk = [("U", c, "hl" if side == 0 else "hr") for c in range(4)]
                self.op(dve, lambda e: e.tensor_scalar(out=dst, in0=self.HG[:, 0, :, src_sl],
                                                       scalar1=self.vec(mname, 0), scalar2=None, op0=ALU.mult),
                        reads=["HG", "vecs"], writes=wk)
                for r in range(1, R):
                    self.op(dve, lambda e, r=r: e.scalar_tensor_tensor(
                        out=dst, in0=self.HG[:, r, :, src_sl], scalar=self.vec(mname, r), in1=dst,
                        op0=ALU.mult, op1=ALU.add), reads=["HG", "vecs"] + wk, writes=wk)
        else:
            self.op(pool, lambda e: e.memset(self.U[:, :, 0:16], 0.0), writes=[("U", c, "hl") for c in range(4)])
            self.op(pool, lambda e: e.memset(self.U[:, :, T + 16:T + 32], 0.0),
                    writes=[("U", c, "hr") for c in range(4)])

    def main_block(self, u, l, blk):
        T, NBLK = self.T, self.NBLK
        act, dve, pool, sp = self.act, self.dve, self.pool, self.sp
        sample = (u == 0)
        cs = slice(blk * NBK, (blk + 1) * NBK)
        vo = lambda name, per, c: self.vec(name, l * per + c)
        hk = [("H", k) for k in range(8)]
        if not self.norm_done:
            self.main_norm(blk, self.A1[:, l, u, :], self.mod[:, l, 0:8, u])
        self.norm_done = False
        self.conv_inline = not self.conv_done
        self.conv_done = False
        if self.conv_inline and not self.dg_prefetched:
            self.conv_build(l, 0)
        self.dg_prefetched = False
        if self.wqa_prefetched:
            wqa, wqk = self.wqa_pre, [("KV", 1), ("KV", 2)]
            self.wqa_prefetched = False
        else:
            s, wqa = self.wload(self.WQA[l][:, :, :], "p (k n) -> p k n", k=8)
            wqk = [("W", s)]
        qf = []
        sbank = self.stat_begin()
        for c in range(3):
            bk = self.bank_alloc()
            self.mm(self.PS[bk][:, :], [(wqa[:, k, c * 128:(c + 1) * 128], self.H[:, k, :]) for k in range(8)],
                    reads=wqk + hk, bank=bk)
            f = self.ftile()
            self.op(act, lambda e: e.activation(out=self.Ft[:, f, :], in_=self.PS[bk][:, :], func=AF.Copy),
                    reads=[("PS", bk)], writes=[("F", f)])
            self.stat_sq(sbank, self.PS[bk][:, :], [("PS", bk)], c, 3)
            self.bank_free(bk)
            qf.append(f)
        if self.conv_inline:
            self.conv_mm(l, blk, 0)
            self.conv_build(l, 1)
        fr = self.stat_rstd(sbank, 384)
        for c in range(3):
            self.op(dve, lambda e: e.scalar_tensor_tensor(
                out=self.QA[:, c, :], in0=self.Ft[:, qf[c], :], scalar=vo("gq", 3, c), in1=self.RT[:, fr, :],
                op0=ALU.mult, op1=ALU.mult), reads=[("F", qf[c]), ("RT", fr), "vecs"], writes=[("QA", c)])
        ri = self.rope_load(u, blk)
        s1, wq = self.wload(self.WQ[l][:, :, :], "p (k n) -> p k n", k=3)
        s2, wqr = self.wload(self.WQR[l][:, :, :], "p (k n) -> p k n", k=3)
        qak = [("QA", c) for c in range(3)]
        for h in range(NH):
            b1 = self.bank_alloc()
            self.mm(self.PS[b1][:, :], [(wq[:, k, h * 128:(h + 1) * 128], self.QA[:, k, :]) for k in range(3)],
                    reads=[("W", s1)] + qak, bank=b1)
            b2 = self.bank_alloc()
            self.mm(self.PS[b2][:, :], [(wqr[:, k, h * 128:(h + 1) * 128], self.QA[:, k, :]) for k in range(3)],
                    reads=[("W", s2)] + qak, bank=b2)
            f1, f2 = self.ftile(), self.ftile()
            self.op(dve, lambda e, f1=f1, b1=b1: e.tensor_tensor(out=self.Ft[:, f1, :], in0=self.PS[b1][:, :],
                                                                in1=self.RC[:, ri, :], op=ALU.mult),
                    reads=[("PS", b1), ("RC", ri)], writes=[("F", f1)])
            self.bank_free(b1)
            self.op(dve, lambda e, f2=f2, b2=b2: e.tensor_tensor(out=self.Ft[:, f2, :], in0=self.PS[b2][:, :],
                                                                in1=self.RS[:, ri, :], op=ALU.mult),
                    reads=[("PS", b2), ("RS", ri)], writes=[("F", f2)])
            self.bank_free(b2)
            self.op(pool, lambda e, h=h, f1=f1, f2=f2: e.tensor_tensor(
                out=self.QT[:, h, :], in0=self.Ft[:, f1, :], in1=self.Ft[:, f2, :], op=ALU.add),
                reads=[("F", f1), ("F", f2)], writes=[("QT", h)])
        self.attention(u, l, blk)
        zk = [("ZS", c) for c in range(4)]
        ak = [("A", c) for c in range(4)]
        for c in range(8):
            s, wm = self.wload(self.WMG[l][c, :, :, :], "p (k n) -> p k n", k=24)
            byc = self.bank_alloc()
            self.mm(self.PS[byc][:, :], [(wm[:, k, :], self.ZS[:, k, :]) for k in range(4)],
                    reads=[("W", s)] + zk, bank=byc)
            bya = self.bank_alloc()
            self.mm(self.PS[bya][:, :], [(wm[:, 4 + k, :], self.A[:, k, :]) for k in range(4)],
                    reads=[("W", s)] + ak, bank=bya)
            bgc = self.bank_alloc()
            self.mm(self.PS[bgc][:, :], [(wm[:, 8 + k, :], self.H[:, k, :]) for k in range(8)],
                    reads=[("W", s)] + hk, bank=bgc)
            bga = self.bank_alloc()
            self.mm(self.PS[bga][:, :], [(wm[:, 16 + k, :], self.H[:, k, :]) for k in range(8)],
                    reads=[("W", s)] + hk, bank=bga)
            f1, f2 = self.ftile(), self.ftile()
            self.op(act, lambda e, f1=f1, bgc=bgc: e.activation(out=self.Ft[:, f1, :], in_=self.PS[bgc][:, :],
                                                               func=AF.Sigmoid), reads=[("PS", bgc)], writes=[("F", f1)])
            self.bank_free(bgc)
            self.op(act, lambda e, f2=f2, bga=bga: e.activation(out=self.Ft[:, f2, :], in_=self.PS[bga][:, :],
                                                               func=AF.Sigmoid), reads=[("PS", bga)], writes=[("F", f2)])
            self.bank_free(bga)
            self.op(dve, lambda e, f1=f1, byc=byc: e.tensor_tensor(out=self.Ft[:, f1, :], in0=self.PS[byc][:, :],
                                                                  in1=self.Ft[:, f1, :], op=ALU.mult),
                    reads=[("PS", byc), ("F", f1)], writes=[("F", f1)])
            self.bank_free(byc)
            self.op(dve, lambda e, f2=f2, bya=bya: e.tensor_tensor(out=self.Ft[:, f2, :], in0=self.PS[bya][:, :],
                                                                  in1=self.Ft[:, f2, :], op=ALU.mult),
                    reads=[("PS", bya), ("F", f2)], writes=[("F", f2)])
            self.bank_free(bya)
            self.op(pool, lambda e, c=c, f1=f1, f2=f2: e.tensor_tensor(
                out=self.MIX[:, c, :], in0=self.Ft[:, f1, :], in1=self.Ft[:, f2, :], op=ALU.add),
                reads=[("F", f1), ("F", f2)], writes=[("MIX", c)])
        mk = [("MIX", c) for c in range(8)]
        for half in range(2):
            s, wo = self.wload(self.WOUT[l][:, :, half * 512:(half + 1) * 512], "p (k n) -> p k n", grp=2, k=8)
            for cc in range(4):
                c = half * 4 + cc
                bk = self.bank_alloc()
                self.mm(self.PS[bk][:, :], [(wo[:, k, cc * 128:(cc + 1) * 128], self.MIX[:, k, :]) for k in range(8)],
                        reads=[("W", s)] + mk, bank=bk)
                self.op(dve, lambda e, c=c, bk=bk: e.scalar_tensor_tensor(
                    out=self.XT[:, c, cs], in0=self.PS[bk][:, :], scalar=self.mod[:, l, 16 + c:17 + c, u],
                    in1=self.XT[:, c, cs], op0=ALU.mult, op1=ALU.add),
                    reads=[("PS", bk), ("mod", l), ("XT", c, blk)], writes=[("XT", c, blk)])
                self.bank_free(bk)
        self.main_norm(blk, self.A2[:, l, u, :], self.mod[:, l, 24:32, u])
        cops = self.conv_dve_ops(l, blk + 1) if blk + 1 < NBLK else []
        per = (len(cops) + 15) // 16

        def conv_slice():
            for _ in range(per):
                if cops:
                    cops.pop(0)()

        for g in range(8):
            s, wu = self.wload(self.WUP[l][:, :, g * 512:(g + 1) * 512], "p (k n) -> p k n", grp=3, k=8)
            for hc in range(4):
                i = g * 4 + hc
                bk = self.bank_alloc()
                self.mm(self.PS[bk][:, :], [(wu[:, k, hc * 128:(hc + 1) * 128], self.H[:, k, :]) for k in range(8)],
                        reads=[("W", s)] + hk, bank=bk)
                f = self.ftile()
                self.op(act, lambda e: e.activation(out=self.Ft[:, f, :], in_=self.PS[bk][:, :], func=AF.Relu),
                        reads=[("PS", bk)], writes=[("F", f)])
                self.bank_free(bk)
                self.op(pool, lambda e: e.tensor_tensor(out=self.HID[:, i, :], in0=self.Ft[:, f, :],
                                                        in1=self.Ft[:, f, :], op=ALU.mult),
                        reads=[("F", f)], writes=[self.hid_key(i)])
            conv_slice()
        if blk + 1 < NBLK:
            self.main_norm(blk + 1, self.A1[:, l, u, :], self.mod[:, l, 0:8, u])
            self.norm_done = True
        hidk = sorted(set(self.hid_key(i) for i in range(32)), key=str)
        for c in range(8):
            s, wd = self.wload(self.WDN[l][c, :, :, :], "p (k n) -> p k n", grp=4, k=32)
            bk = self.bank_alloc()
            self.mm(self.PS[bk][:, :], [(wd[:, k, :], self.HID[:, k, :]) for k in range(32)],
                    reads=[("W", s)] + hidk, bank=bk)
            self.op(dve, lambda e: e.scalar_tensor_tensor(
                out=self.XT[:, c, cs], in0=self.PS[bk][:, :], scalar=self.mod[:, l, 40 + c:41 + c, u],
                in1=self.XT[:, c, cs], op0=ALU.mult, op1=ALU.add),
                reads=[("PS", bk), ("mod", l), ("XT", c, blk)], writes=[("XT", c, blk)])
            self.bank_free(bk)
            conv_slice()
        if blk + 1 < NBLK:
            while cops:
                cops.pop(0)()
            self.conv_ln(l)
            self.conv_done = True

    @staticmethod
    def hid_key(i):
        if i < 16:
            return ("KV", i // 4)
        if i < 24:
            return ("QT", i - 16)
        return ("MIX", i - 24)

    def conv_build(self, l, c):
        self.dma(self.sp, self.d_dgl, self.DG[:, 0:3968], self.WDG[l][c, :, :], writes=["DG"], extra=[self.dg_tok])

    def conv_mm(self, l, blk, c):
        NBLK = self.NBLK
        rk = [("U", c, blk), ("U", c, blk - 1) if blk > 0 else ("U", c, "hl"),
              ("U", c, blk + 1) if blk < NBLK - 1 else ("U", c, "hr")]
        bk = self.bank_alloc()
        pairs = [(self.DG[:, k * 128:(k + 1) * 128], self.U[:, c, blk * NBK + 1 + k:blk * NBK + 1 + k + NBK])
                 for k in range(31)]
        self.mm(self.PS[bk][:, :], pairs, reads=["DG"] + rk, bank=bk)
        self.op(self.act, lambda e: e.activation(out=self.ZS[:, c, :], in_=self.PS[bk][:, :], func=AF.Identity,
                                                 bias=self.vec("cb", l * 4 + c)),
                reads=[("PS", bk), "vecs"], writes=[("ZS", c)])
        self.bank_free(bk)

    def conv_ln(self, l):
        act, dve, pool = self.act, self.dve, self.pool
        bm = self.bank_alloc()
        bq = self.bank_alloc()
        for c in range(4):
            self.mm_step(self.PS[bm][:, :], self.ones[:], self.ZS[:, c, :], c == 0, c == 3, ["ones", ("ZS", c)], bm)
            self.stat_sq(bq, self.ZS[:, c, :], [("ZS", c)], c, 4)
        fm, fv = self.rtile(), self.rtile()
        self.op(act, lambda e: e.activation(out=self.RT[:, fm, :], in_=self.PS[bm][:, :], func=AF.Copy,
                                            scale=1.0 / 512), reads=[("PS", bm)], writes=[("RT", fm)])
        self.bank_free(bm)
        self.op(dve, lambda e: e.tensor_tensor(out=self.RT[:, fv, :], in0=self.RT[:, fm, :], in1=self.RT[:, fm, :],
                                               op=ALU.mult), reads=[("RT", fm)], writes=[("RT", fv)])
        self.op(dve, lambda e: e.scalar_tensor_tensor(out=self.RT[:, fv, :], in0=self.PS[bq][:, :], scalar=1.0 / 512,
                                                      in1=self.RT[:, fv, :], op0=ALU.mult, op1=ALU.subtract),
                reads=[("PS", bq), ("RT", fv)], writes=[("RT", fv)])
        self.bank_free(bq)
        self.op(act, lambda e: e.activation(out=self.RT[:, fv, :], in_=self.RT[:, fv, :], func=AF.Ln,
                                            bias=self.epsb[:, 0:1]), reads=[("RT", fv), "epsb"], writes=[("RT", fv)])
        self.op(act, lambda e: e.activation(out=self.RT[:, fv, :], in_=self.RT[:, fv, :], func=AF.Exp, scale=-0.5),
                reads=[("RT", fv)], writes=[("RT", fv)])
        for c in range(4):
            f = self.ftile()
            self.op(dve, lambda e: e.tensor_tensor(out=self.Ft[:, f, :], in0=self.ZS[:, c, :], in1=self.RT[:, fm, :],
                                                   op=ALU.subtract), reads=[("ZS", c), ("RT", fm)], writes=[("F", f)])
            self.op(pool, lambda e: e.tensor_tensor(out=self.Ft[:, f, :], in0=self.Ft[:, f, :], in1=self.RT[:, fv, :],
                                                    op=ALU.mult), reads=[("F", f), ("RT", fv)], writes=[("F", f)])
            self.op(act, lambda e: e.activation(out=self.ZS[:, c, :], in_=self.Ft[:, f, :], func=AF.Silu,
                                                scale=self.vec("lg", l * 4 + c), bias=self.vec("lb", l * 4 + c)),
                    reads=[("F", f), "vecs"], writes=[("ZS", c)])

    def conv_dve_ops(self, l, blk):
        NBLK = self.NBLK
        accA = self.PT[:, 0:2, :].rearrange("p a n -> p (a n)").bitcast(F32)
        accB = self.QA[:, 0:2, :].rearrange("p a n -> p (a n)").bitcast(F32)
        ka = [("PT", 0), ("PT", 1)]
        kb = [("QA", 0), ("QA", 1)]
        ops = []
        for c in range(4):
            rk = [("U", c, blk), ("U", c, blk - 1) if blk > 0 else ("U", c, "hl"),
                  ("U", c, blk + 1) if blk < NBLK - 1 else ("U", c, "hr"), "vecs"]
            for k in range(31):
                src = self.U[:, c, blk * NBK + 1 + k:blk * NBK + 1 + k + NBK]
                wcol = self.vec("cw", (l * 4 + c) * 31 + k)
                if k == 0:
                    ops.append(lambda src=src, wcol=wcol, c=c, rk=rk: self.op(self.dve, lambda e: e.tensor_scalar(
                        out=accA, in0=src, scalar1=wcol, scalar2=self.vec("cb", l * 4 + c), op0=ALU.mult,
                        op1=ALU.add), reads=rk, writes=ka))
                elif k == 1:
                    ops.append(lambda src=src, wcol=wcol, rk=rk: self.op(self.dve, lambda e: e.tensor_scalar(
                        out=accB, in0=src, scalar1=wcol, scalar2=None, op0=ALU.mult), reads=rk, writes=kb))
                else:
                    acc, kk = (accA, ka) if k % 2 == 0 else (accB, kb)
                    ops.append(lambda src=src, wcol=wcol, rk=rk, acc=acc, kk=kk: self.op(
                        self.dve, lambda e: e.scalar_tensor_tensor(out=acc, in0=src, scalar=wcol, in1=acc,
                                                                   op0=ALU.mult, op1=ALU.add),
                        reads=rk + kk, writes=kk))
            ops.append(lambda c=c: self.op(self.pool, lambda e: e.tensor_tensor(
                out=self.ZS[:, c, :], in0=accA, in1=accB, op=ALU.add), reads=ka + kb, writes=[("ZS", c)]))
        return ops

    def att_hook(self, l, blk, h):
        if not self.conv_inline:
            return
        if h < 3:
            self.conv_mm(l, blk, h + 1)
            if h < 2:
                self.conv_build(l, h + 2)
        elif h == 3:
            self.conv_ln(l)

    def attention(self, u, l, blk):
        act, dve, sp, pe = self.act, self.dve, self.sp, self.pe
        sample = (u == 0)
        KP = self.KP
        nkc = KP // 128
        if sample:
            pieces = [(r, p) for r in range(R) for p in range(self.T // KP)]
        else:
            pieces = [(None, p) for p in range(self.T // KP)]
        steps = []
        for h in range(NH):
            for pi, (r, p) in enumerate(pieces):
                for kc in range(nkc):
                    steps.append(dict(h=h, pi=pi, r=r, p=p, kc=kc, first=(pi == 0 and kc == 0),
                                      last=(pi == len(pieces) - 1 and kc == nkc - 1)))
        pbuf = {}
        accb = {}

        def issue_loads(st):
            h = st["h"]
            if st["kc"] == 0:
                i = self.kv_rr
                self.kv_rr ^= 1
                pbuf[(h, st["pi"])] = i
                r, p = st["r"], st["p"]
                if r is None:
                    base, ro, rkeys = self.GI, h * 256, self.gi_keys
                else:
                    base, ro, rkeys = self.GO, h * 1024 + r * 256, [("GO", h)]
                ksrc = base[ro:ro + 128, p * KP:(p + 1) * KP]
                vsrc = base[ro + 128:ro + 256, p * KP:(p + 1) * KP]
                self.dma(sp, self.d_kv[i], self.Kb[i][:, 0:KP], ksrc, reads=rkeys, writes=[("KV", 2 * i)])
                self.dma(sp, self.d_vv[i], self.Vb[i][:, 0:nkc, :].rearrange("p k e -> p (k e)"), vsrc,
                         reads=rkeys, writes=[("KV", 2 * i + 1)])
            st["buf"] = pbuf[(h, st["pi"])]

        def emit_Spair(sa, sb2):
            issue_loads(sa)
            issue_loads(sb2)
            b0 = self.pair_alloc()
            for st, bk in ((sa, b0), (sb2, b0 + 1)):
                i, kc, h = st["buf"], st["kc"], st["h"]
                self.mm(self.PS[bk][:, :], [(self.Kb[i][:, kc * 128:(kc + 1) * 128], self.QT[:, h, :])],
                        reads=[("KV", 2 * i), ("QT", h)], bank=bk)
            q = self.ptp_rr
            self.ptp_rr ^= 1
            sa["pt"], sb2["pt"] = 2 * q, 2 * q + 1
            self.op(act, lambda e: e.activation(out=self.PT[:, 2 * q:2 * q + 2, :], in_=self.PSall[:, b0:b0 + 2, :],
                                                func=AF.Exp, scale=ATT_SCALE),
                    reads=[("PS", b0), ("PS", b0 + 1)], writes=[("PT", 2 * q), ("PT", 2 * q + 1)])
            self.bank_free(b0)
            self.bank_free(b0 + 1)

        def emit_PV(st):
            h, i, kc, pt = st["h"], st["buf"], st["kc"], st["pt"]
            if st["first"]:
                accb[h] = self.bank_alloc(lo=4)
            acc = accb[h]
            accp = self.PS[acc]
            self.mm_step(accp[:, :], self.Vb[i][:, kc, :], self.PT[:, pt, :], st["first"], st["last"],
                         [("KV", 2 * i + 1), ("PT", pt)], acc)
            if st["last"]:
                f = self.ftile()
                self.op(dve, lambda e: e.reciprocal(out=self.Ft[0:64, f, :], in_=accp[64:128, :]),
                        reads=[("PS", acc)], writes=[("F", f)])
                po = (h % 2) * 64
                self.op(dve, lambda e: e.tensor_tensor(out=self.A[po:po + 64, h // 2, :], in0=accp[0:64, :],
                                                       in1=self.Ft[0:64, f, :], op=ALU.mult),
                        reads=[("PS", acc), ("F", f)], writes=[("A", h // 2)])
                self.bank_free(acc)
                self.att_hook(l, blk, h)

        npairs = len(steps) // 2
        for j in range(npairs + 1):
            if j < npairs:
                emit_Spair(steps[2 * j], steps[2 * j + 1])
            if j >= 1:
                emit_PV(steps[2 * j - 2])
                emit_PV(steps[2 * j - 1])

    def final_out_block(self, u, blk):
        T, NBLK = self.T, self.NBLK
        act, dve, pool, pe = self.act, self.dve, self.pool, self.pe
        YT = self.R32[:, 0:8192].bitcast(F32).rearrange("p (c n) -> p c n", n=NBK)
        if True:
            cs = slice(blk * NBK, (blk + 1) * NBK)
            sbank = self.stat_begin()
            for c in range(8):
                self.stat_sq(sbank, self.XT[:, c, cs], [("XT", c, blk)], c, 8)
            f = self.stat_rstd(sbank, D)
            for c in range(8):
                self.op(dve, lambda e, c=c: e.scalar_tensor_tensor(
                    out=YT[:, c, :], in0=self.XT[:, c, cs], scalar=self.vec("fg", c), in1=self.RT[:, f, :],
                    op0=ALU.mult, op1=ALU.mult), reads=[("XT", c, blk), ("RT", f), "vecs"], writes=[("KV", c // 2)])
            ytk = [("KV", j) for j in range(4)]
            for t4 in range(4):
                tt = blk * 4 + t4
                i = self.out_rr
                self.out_rr ^= 1
                st = self.stage32(i)
                keys = [("MIX", 4 * i + j) for j in range(4)]
                for half in range(2):
                    bank = self.bank_alloc()
                    ps = self.PS[bank]
                    for t in self.tr.deps(ytk + ["ident"], [("PS", bank)]):
                        pe.wait(t)
                    ins = None
                    for j in range(4):
                        c = half * 4 + j
                        ins = pe.e.transpose(out=ps[:, j * 128:(j + 1) * 128], in_=YT[:, c, t4 * 128:(t4 + 1) * 128],
                                             identity=self.ident[:])
                    tok = pe.sig(ins)
                    self.tr.record(tok, ytk + ["ident"], [("PS", bank)])
                    dst = st[:, half * 512:(half + 1) * 512]
                    hk_ = keys[half * 2:half * 2 + 2]
                    if half == 0:
                        self.op(act, lambda e, dst=dst, ps=ps: e.activation(out=dst, in_=ps[:, :], func=AF.Copy),
                                reads=[("PS", bank)], writes=hk_)
                    else:
                        self.op(dve, lambda e, dst=dst, ps=ps: e.tensor_copy(out=dst, in_=ps[:, :]),
                                reads=[("PS", bank)], writes=hk_)
                    self.bank_free(bank)
                self.dma(pool, self.d_out[i], self.ys[u, tt * 128:(tt + 1) * 128, :], st, reads=keys)


def _make_program(T, NU):
    p = Prog(T, NU)
    p.kv_rr = 0
    p.pt_rr = 0
    p.out_rr = 0
    p.rope_rr = 0
    p.l = 0
    p.norm_done = False
    p.dg_prefetched = False
    p.wqa_prefetched = False
    p.conv_done = False
    p.conv_inline = True
    return p.build()


def _rope_tables(pos0, T):
    inv = (np.float32(10000.0) ** (-np.arange(0, 64, 2, dtype=np.float32) / np.float32(64))).astype(np.float32)
    ang = (np.arange(pos0, pos0 + T, dtype=np.float32)[:, None] * inv[None, :]).astype(np.float32)
    cos = np.cos(ang).astype(np.float32).T
    sin = np.sin(ang).astype(np.float32).T
    C = np.ones((128, T), np.float32)
    S = np.zeros((128, T), np.float32)
    C[64:96] = cos
    C[96:128] = cos
    S[64:96] = -sin
    S[96:128] = sin
    return C, S


def _pm(v, n):
    v = np.asarray(v, np.float32)
    lead = v.shape[:-1]
    return np.moveaxis(v.reshape(lead + (n, 128)), -1, 0)


def run(inputs, T, NPC):
    NU = NPC + 1
    f32 = lambda a: np.ascontiguousarray(np.asarray(a, dtype=np.float32))
    xp, xsm = f32(inputs["x_prompt"]), f32(inputs["x_sample"])
    cp, csm = f32(inputs["c_prompt"]), f32(inputs["c_sample"])
    nc = _make_program(T, NU)
    vec_common = np.zeros((128, NV), np.float32)

    def put(name, arr):
        arr = np.asarray(arr, np.float32).reshape(128, -1)
        vec_common[:, _VOFF[name]:_VOFF[name] + arr.shape[1]] = arr

    put("ada_b", _pm(inputs["ada_b"], 48))
    put("g1", _pm(inputs["norm_mix_g"], 8))
    put("g2", _pm(inputs["norm_mlp_g"], 8))
    put("fg", _pm(inputs["final_g"], 8))
    put("gq", _pm(inputs["q_norm_g"], 3))
    put("gkv", _pm(inputs["kv_norm_g"], 2))
    put("cb", _pm(inputs["conv_dw_b"], 4))
    put("lg", _pm(inputs["conv_ln_g"], 4))
    put("lb", _pm(inputs["conv_ln_b"], 4))
    cw = np.asarray(inputs["conv_dw"], np.float32)
    cw = cw.reshape(2, 31, 4, 128).transpose(3, 0, 2, 1)
    put("cw", cw)
    Cp, Sp = _rope_tables(0, T)
    shared = {k: f32(inputs[k]) for k in ["ada_w", "w_in", "w_q_up", "w_kv_up", "w_attn_o", "w_conv_out",
                                          "w_out", "w_mlp_up", "w_mlp_down"]}
    in_maps = []
    for core in range(NCORES):
        s, q = core // R, core % R
        xs = np.empty((NU, T, D), np.float32)
        xs[0] = xsm[s, q * T:(q + 1) * T]
        xs[1:] = xp[core * NPC:(core + 1) * NPC]
        c = np.concatenate([csm[s:s + 1], cp[core * NPC:(core + 1) * NPC]], axis=0)
        cT = np.ascontiguousarray(c.reshape(NU, 8, 128).transpose(2, 1, 0))
        vecs = vec_common.copy()
        if q > 0:
            vecs[:, _VOFF["mL"] + q - 1] = 1.0
        if q < R - 1:
            vecs[:, _VOFF["mR"] + q + 1] = 1.0
        Cs, Ss = _rope_tables(q * T, T)
        m = {"xs": xs, "cT": cT, "vecs": vecs,
             "ropeC": np.ascontiguousarray(np.stack([Cs, Cp])), "ropeS": np.ascontiguousarray(np.stack([Ss, Sp]))}
        m.update(shared)
        in_maps.append(m)
    res = run_bass_kernel_spmd(nc, in_maps, core_ids=list(range(NCORES)))
    y_prompt = np.empty_like(xp)
    y_sample = np.empty_like(xsm)
    for core in range(NCORES):
        s, q = core // R, core % R
        ys = res.results[core]["ys"]
        y_sample[s, q * T:(q + 1) * T] = ys[0]
        y_prompt[core * NPC:(core + 1) * NPC] = ys[1:]
    return y_prompt, y_sample


def kernel(**inputs):
    return run(inputs, 2048, 4)
```

```python
import numpy as np
from contextlib import ExitStack

import concourse.bass as bass
import concourse.mybir as mybir
from concourse.bass_utils import run_bass_kernel_spmd

F32 = mybir.dt.float32
BF16 = mybir.dt.bfloat16
AF = mybir.ActivationFunctionType
ALU = mybir.AluOpType

D = 1024
NBK = 512
NH = 8
R = 4
NCORES = 8
EPS = 1e-6
ATT_SCALE = float(128 ** -0.5)

_VOFF = {}
_o = 0
for _name, _n in [("ada_b", 96), ("g1", 16), ("g2", 16), ("fg", 8), ("gq", 6), ("gkv", 4),
                  ("cb", 8), ("lg", 8), ("lb", 8), ("cw", 248), ("mL", R), ("mR", R)]:
    _VOFF[_name] = _o
    _o += _n
NV = _o


class Tok:
    __slots__ = ("sem", "val", "ds")

    def __init__(self, sem, val, ds=None):
        self.sem = sem
        self.val = val
        self.ds = ds


class Eng:
    def __init__(self, nc, eng, name, es):
        self.nc = nc
        self.e = eng
        self.name = name
        self.sem = es.enter_context(nc.semaphore("s_" + name))
        self.count = 0
        self.waited = {}

    def wait(self, tok):
        if tok is None:
            return
        k = id(tok.sem)
        val = tok.val if tok.ds is None else tok.ds.count
        if self.waited.get(k, 0) >= val:
            return
        self.e.wait_ge(tok.sem, val)
        self.waited[k] = val

    def sig(self, ins):
        self.count += 1
        ins.then_inc(self.sem, 1)
        return Tok(self.sem, self.count)


class DSem:
    def __init__(self, nc, name, es, shared=False):
        self.sem = es.enter_context(nc.semaphore("d_" + name))
        self.count = 0
        self.shared = shared

    def sig(self, ins):
        self.count += 16
        ins.then_inc(self.sem, 16)
        return Tok(self.sem, self.count, self if self.shared else None)


class Tracker:
    def __init__(self):
        self.w = {}
        self.r = {}

    def deps(self, reads, writes):
        out = []
        for k in reads:
            t = self.w.get(k)
            if t is not None:
                out.append(t)
        for k in writes:
            t = self.w.get(k)
            if t is not None:
                out.append(t)
            out.extend(self.r.get(k, ()))
        return out

    def record(self, tok, reads, writes):
        for k in reads:
            self.r.setdefault(k, []).append(tok)
        for k in writes:
            self.w[k] = tok
            self.r[k] = []


class Prog:
    def __init__(self, T, NU):
        self.T = T
        self.NU = NU
        self.NBLK = T // NBK
        self.KP = min(T, 2048)
        self.HR = 16384 // T
        self.GR = 2048 + self.HR
        self.nc = bass.Bass("TRN2", target_bir_lowering=False)
        self.tr = Tracker()
        nb = self.NBLK
        self.gi_keys = ([("GI", "K", b) for b in range(nb)] + [("GI", "R", b, h) for b in range(nb) for h in range(8)]
                        + [("GI", "V", b, t) for b in range(nb) for t in range(4)] + [("GI", "H")])

    def op(self, eng, fn, reads=(), writes=()):
        for t in self.tr.deps(reads, writes):
            eng.wait(t)
        tok = eng.sig(fn(eng.e))
        self.tr.record(tok, reads, writes)
        return tok

    def dma(self, eng, dsem, out, in_, reads=(), writes=(), extra=()):
        for t in self.tr.deps(reads, writes):
            eng.wait(t)
        for t in extra:
            eng.wait(t)
        tok = dsem.sig(eng.e.dma_start(out=out, in_=in_))
        self.tr.record(tok, reads, writes)
        return tok

    def mm(self, out_ap, pairs, reads, bank, extra=()):
        pe = self.pe
        writes = (("PS", bank),)
        for t in self.tr.deps(reads, writes):
            pe.wait(t)
        for t in extra:
            pe.wait(t)
        n = len(pairs)
        ins = None
        for i, (l, r) in enumerate(pairs):
            ins = pe.e.matmul(out_ap, lhsT=l, rhs=r, start=(i == 0), stop=(i == n - 1))
        tok = pe.sig(ins)
        self.tr.record(tok, reads, writes)
        return tok

    def mm_step(self, out_ap, lhsT, rhs, first, last, reads, bank):
        pe = self.pe
        wk = (("PS", bank),)
        for t in self.tr.deps(reads, wk if first else ()):
            pe.wait(t)
        tok = pe.sig(pe.e.matmul(out_ap, lhsT=lhsT, rhs=rhs, start=first, stop=last))
        if first:
            self.tr.record(tok, reads, wk)
        else:
            self.tr.record(tok, reads, ())
            self.tr.w[("PS", bank)] = tok
        return tok

    def bank_alloc(self, lo=0):
        for _ in range(16):
            b = self.bank_rr
            self.bank_rr = (self.bank_rr + 1) % 8
            if b >= lo and b not in self.bank_held:
                self.bank_held.add(b)
                return b
        raise RuntimeError("no free PSUM bank")

    def pair_alloc(self):
        for _ in range(4):
            b = 2 * self.pair_rr
            self.pair_rr = (self.pair_rr + 1) % 2
            if b not in self.bank_held and (b + 1) not in self.bank_held:
                self.bank_held.add(b)
                self.bank_held.add(b + 1)
                return b
        for b in (4, 6):
            if b not in self.bank_held and (b + 1) not in self.bank_held:
                self.bank_held.add(b)
                self.bank_held.add(b + 1)
                return b
        raise RuntimeError("no free PSUM bank pair")

    def bank_free(self, b):
        self.bank_held.discard(b)

    def ftile(self):
        i = self.f_rr
        self.f_rr = (self.f_rr + 1) % self.NF
        return i

    def btile(self):
        i = self.b_rr
        self.b_rr = (self.b_rr + 1) % self.NBT
        return i

    def wslot(self):
        i = self.w_rr
        self.w_rr = (self.w_rr + 1) % 3
        return i

    def rtile(self):
        i = self.rt_rr
        self.rt_rr ^= 1
        return i

    def vec(self, name, col):
        o = _VOFF[name] + col
        return self.vecs[:, o:o + 1]

    def build(self):
        nc = self.nc
        T, NU, NBLK = self.T, self.NU, self.NBLK
        GR = self.GR
        with ExitStack() as es:
            self.es = es
            dt = nc.dram_tensor
            self.xs = dt("xs", [NU, T, D], F32, kind="ExternalInput").ap()
            self.ys = dt("ys", [NU, T, D], F32, kind="ExternalOutput").ap()
            self.cT_d = dt("cT", [128, 8, NU], F32, kind="ExternalInput").ap()
            self.vecs_d = dt("vecs", [128, NV], F32, kind="ExternalInput").ap()
            self.ropeC_d = dt("ropeC", [2, 128, T], F32, kind="ExternalInput").ap()
            self.ropeS_d = dt("ropeS", [2, 128, T], F32, kind="ExternalInput").ap()
            self.ada_w = dt("ada_w", [2, D, 6 * D], F32, kind="ExternalInput").ap()
            self.w_in = dt("w_in", [2, D, 3776], F32, kind="ExternalInput").ap()
            self.w_q_up = dt("w_q_up", [2, 384, 1024], F32, kind="ExternalInput").ap()
            self.w_kv_up = dt("w_kv_up", [2, 256, 1024], F32, kind="ExternalInput").ap()
            self.w_attn_o = dt("w_attn_o", [2, 512, 1024], F32, kind="ExternalInput").ap()
            self.w_conv_out = dt("w_conv_out", [2, 512, 1024], F32, kind="ExternalInput").ap()
            self.w_out = dt("w_out", [2, D, D], F32, kind="ExternalInput").ap()
            self.w_up = dt("w_mlp_up", [2, D, 4 * D], F32, kind="ExternalInput").ap()
            self.w_dn = dt("w_mlp_down", [2, 4 * D, D], F32, kind="ExternalInput").ap()
            self.WPRE = [dt(f"WPRE{l}", [128, 8, 1536], BF16).ap() for l in range(2)]
            self.WKV = [dt(f"WKV{l}", [128, 2, 1024], BF16).ap() for l in range(2)]
            self.WQA = [dt(f"WQA{l}", [128, 8, 384], BF16).ap() for l in range(2)]
            self.WQ = [dt(f"WQ{l}", [128, 3, 1024], BF16).ap() for l in range(2)]
            self.WQR = [dt(f"WQR{l}", [128, 3, 1024], BF16).ap() for l in range(2)]
            self.WMG = [dt(f"WMG{l}", [8, 128, 24, 128], BF16).ap() for l in range(2)]
            self.WOUT = [dt(f"WOUT{l}", [128, 8, 1024], BF16).ap() for l in range(2)]
            self.WUP = [dt(f"WUP{l}", [128, 8, 4096], BF16).ap() for l in range(2)]
            self.WDN = [dt(f"WDN{l}", [8, 128, 32, 128], BF16).ap() for l in range(2)]
            self.WDG = [dt(f"WDG{l}", [4, 128, 3968], BF16).ap() for l in range(2)]
            self.GI_t = dt("GI", [GR, T], BF16)
            self.GO_t = dt("GO", [R * GR, T], BF16)
            self.GI = self.GI_t.ap()
            self.GO = self.GO_t.ap()

            sb = lambda name, shape, dtp: es.enter_context(nc.sbuf_tensor(name, shape, dtp))
            self.XT = sb("XT", [128, 8, T], F32)
            self.U = sb("U", [128, 4, T + 32], BF16)
            self.Wb = sb("Wb", [128, 3, 4096], BF16)
            self.DG = sb("DG", [128, 4096], BF16)
            self.identb = sb("identb", [128, 128], BF16)
            self.RT = sb("RT", [128, 2, NBK], F32)
            self.rt_rr = 0
            self.H = sb("H", [128, 8, NBK], BF16)
            self.QA = sb("QA", [128, 3, NBK], BF16)
            self.A = sb("A", [128, 4, NBK], BF16)
            self.ZS = sb("ZS", [128, 4, NBK], BF16)
            self.PT = sb("PT", [128, 4, NBK], BF16)
            self.R32 = sb("R32", [128, 16384], BF16)
            self.NF = 8
            self.NBT = 4
            self.Ft = sb("Ft", [128, self.NF, NBK], F32)
            self.Bt = sb("Bt", [128, self.NBT, NBK], BF16)
            self.RC = sb("RC", [128, 2, NBK], F32)
            self.RS = sb("RS", [128, 2, NBK], F32)
            self.ident = sb("ident", [128, 128], F32)
            self.ones = sb("ones", [128, 128], BF16)
            self.zeros = sb("zeros", [128, 512], BF16)
            self.epsb = sb("epsb", [128, 1], F32)
            self.vecs = sb("vecs_sb", [128, NV], F32)
            self.cTs = sb("cTs", [128, 8, NU], F32)
            self.scT = sb("scT", [128, 8, NU], BF16)
            self.mod = sb("mod", [128, 2, 48, NU], F32)
            self.A1 = sb("A1", [128, 2, NU, 8], F32)
            self.A2 = sb("A2", [128, 2, NU, 8], F32)
            self.Hs = sb("Hs", [128, 4, 32], BF16)
            self.HG = sb("HG", [128, R, 4, 32], BF16)
            R32 = self.R32
            self.Kb = [R32[:, 0:2048], R32[:, 4096:6144]]
            self.Vb = [R32[:, 2048:4096].rearrange("p (k e) -> p k e", e=128),
                       R32[:, 6144:8192].rearrange("p (k e) -> p k e", e=128)]
            self.QT = R32[:, 8192:12288].rearrange("p (h n) -> p h n", n=NBK)
            self.MIX = R32[:, 12288:16384].rearrange("p (h n) -> p h n", n=NBK)
            self.HID = R32[:, :].rearrange("p (h n) -> p h n", n=NBK)
            self.KVA = R32[:, 0:1024].rearrange("p (c n) -> p c n", n=NBK)
            self.KRt = R32[:, 1024:1536]
            self.KTs = self.QT
            self.Vs = R32[:, 12288:16384].rearrange("p (t h e) -> p t h e", t=4, h=8)
            self.PSall = es.enter_context(nc.psum_tensor("psall", [128, 8, NBK], F32))
            self.PS = [self.PSall[:, i, :] for i in range(8)]
            self.pair_rr = 0
            self.ptp_rr = 0
            self.bank_rr = 0
            self.bank_held = set()
            self.f_rr = 0
            self.b_rr = 0
            self.w_rr = 0
            self.pe = Eng(nc, nc.tensor, "pe", es)
            self.act = Eng(nc, nc.scalar, "act", es)
            self.dve = Eng(nc, nc.vector, "dve", es)
            self.pool = Eng(nc, nc.gpsimd, "pool", es)
            self.sp = Eng(nc, nc.sync, "sp", es)
            ds = lambda name: DSem(nc, name, es)
            self.d_w = [ds(f"w{i}") for i in range(4)]
            self.d_kv = [ds(f"kv{i}") for i in range(2)]
            self.d_vv = [ds(f"vv{i}") for i in range(2)]
            self.d_rope = [ds(f"rope{i}") for i in range(2)]
            self.d_ropes = [ds(f"ropes{i}") for i in range(2)]
            self.d_c = ds("cload")
            self.d_wqa = ds("wqa")
            self.d_ada = [ds(f"ada{i}") for i in range(3)]
            self.d_x = [ds(f"x{i}") for i in range(2)]
            self.d_misc = ds("misc")
            self.d_pre = [[ds(f"pc{l}{g}") for g in range(5)] for l in range(2)]
            self.d_dg = ds("dg")
            self.d_dgl = ds("dgl")
            self.d_st = DSem(nc, "store", es, shared=True)
            self.d_out = [ds(f"o{i}") for i in range(2)]
            self.cc_sem = [es.enter_context(nc.semaphore(f"cc{i}")) for i in range(9)]
            self.cc_count = [0] * 9

            self.stop = 0
            self.startup()
            for u in range(NU):
                if self.stop == 1:
                    break
                self.unit(u)
                if self.stop:
                    break
            for i in range(2):
                if self.d_out[i].count:
                    self.sp.wait(Tok(self.d_out[i].sem, self.d_out[i].count))
        return nc

    def startup(self):
        nc = self.nc
        NU = self.NU
        pool, act, dve, pe, sp = self.pool, self.act, self.dve, self.pe, self.sp
        self.op(pool, lambda e: e.memset(self.ident[:], 0.0), writes=["ident"])
        self.op(pool, lambda e: e.affine_select(out=self.ident[:], in_=self.ident[:], pattern=[[-1, 128]],
                                                compare_op=ALU.not_equal, fill=1.0, base=0,
                                                channel_multiplier=1), reads=["ident"], writes=["ident"])
        self.op(pool, lambda e: e.tensor_copy(out=self.identb[:], in_=self.ident[:]), reads=["ident"],
                writes=["identb"])
        self.op(pool, lambda e: e.memset(self.ones[:], 1.0), writes=["ones"])
        self.op(pool, lambda e: e.memset(self.zeros[:], 0.0), writes=["zeros"])
        self.op(pool, lambda e: e.memset(self.epsb[:], EPS), writes=["epsb"])
        self.dma(sp, self.d_misc, self.vecs[:], self.vecs_d[:, :], writes=["vecs"])
        self.dma(sp, self.d_c, self.cTs[:], self.cT_d[:, :, :], writes=["cTs"])
        self.op(act, lambda e: e.activation(out=self.scT[:], in_=self.cTs[:], func=AF.Silu),
                reads=["cTs"], writes=["scT"])
        for l in range(2):
            for c in range(4):
                for k in range(31):
                    self.op(dve, lambda e: e.tensor_scalar(
                        out=self.DG[:, k * 128:(k + 1) * 128], in0=self.identb[:],
                        scalar1=self.vec("cw", (l * 4 + c) * 31 + k), scalar2=None, op0=ALU.mult),
                        reads=["identb", "vecs"], writes=["DG"])
                self.dma(sp, self.d_dg, self.WDG[l][c, :, :], self.DG[:, 0:3968], reads=["DG"])
        self.dg_tok = Tok(self.d_dg.sem, self.d_dg.count)
        self.pre_tok = [[None] * 5, [None] * 5]
        self.precast(0, 0)
        self.ada(0)
        for g in range(1, 5):
            self.precast(0, g)
        self.deferred = [lambda: self.ada(1), lambda: self.precast(1, 0), lambda: self.precast(1, 1),
                         lambda: self.precast(1, 2), lambda: self.precast(1, 3), lambda: self.precast(1, 4)]

    def run_deferred(self, n):
        for _ in range(n):
            if self.deferred:
                self.deferred.pop(0)()

    def ada(self, l):
        NU = self.NU
        pool, dve = self.pool, self.dve
        if True:
            bank = self.bank_alloc()
            ps = self.PS[bank]
            for piece in range(12):
                s = self.wslot()
                wv = self.Wb[:, s, :].rearrange("p (k n) -> p k n", k=8)
                src = self.ada_w[l, :, piece * 512:(piece + 1) * 512].rearrange("(k p) n -> p k n", p=128)
                self.dma(pool, self.d_ada[s], wv, src, writes=[("W", s)])
                for j in range(4):
                    col = (piece * 4 + j) * 8
                    pairs = [(wv[:, k, j * 128:(j + 1) * 128], self.scT[:, k, :]) for k in range(8)]
                    self.mm(ps[:, col:col + NU], pairs, reads=[("W", s), "scT"], bank=bank)
            psv = ps[:, 0:384].rearrange("p (j u) -> p j u", u=8)
            for u in range(NU):
                self.op(dve, lambda e, u=u: e.tensor_tensor(
                    out=self.mod[:, l, :, u], in0=psv[:, :, u],
                    in1=self.vecs[:, _VOFF["ada_b"] + l * 48:_VOFF["ada_b"] + (l + 1) * 48], op=ALU.add),
                    reads=[("PS", bank), "vecs"], writes=[("mod", l)])
            self.bank_free(bank)
            for u in range(NU):
                self.op(dve, lambda e, u=u: e.scalar_tensor_tensor(
                    out=self.A1[:, l, u, :], in0=self.mod[:, l, 8:16, u], scalar=1.0,
                    in1=self.vecs[:, _VOFF["g1"] + l * 8:_VOFF["g1"] + (l + 1) * 8],
                    op0=ALU.add, op1=ALU.mult), reads=[("mod", l), "vecs"], writes=[("A1", l)])
                self.op(dve, lambda e, u=u: e.scalar_tensor_tensor(
                    out=self.A2[:, l, u, :], in0=self.mod[:, l, 32:40, u], scalar=1.0,
                    in1=self.vecs[:, _VOFF["g2"] + l * 8:_VOFF["g2"] + (l + 1) * 8],
                    op0=ALU.add, op1=ALU.mult), reads=[("mod", l), "vecs"], writes=[("A2", l)])

    def precast(self, l, grp):
        pool = self.pool
        cast = lambda out, in_, reads=(): self.dma(pool, self.d_pre[l][grp], out, in_, reads=reads)
        getattr(self, "_precast%d" % grp)(l, cast)
        self.pre_tok[l][grp] = Tok(self.d_pre[l][grp].sem, self.d_pre[l][grp].count)

    def _precast0(self, l, cast):
        kp = lambda ap: ap.rearrange("(k p) n -> p k n", p=128)
        w_in = self.w_in
        z = self.zeros
        WPRE = self.WPRE[l]
        cast(WPRE[:, :, 0:1024], kp(w_in[l, :, 0:1024]))
        cast(WPRE[:, :, 1024:1280], kp(w_in[l, :, 1408:1664]))
        for k in range(8):
            cast(WPRE[:, k, 1280:1344], z[:, 0:64], reads=["zeros"])
            cast(WPRE[:, k, 1408:1472], z[:, 0:64], reads=["zeros"])
        cast(WPRE[:, :, 1344:1408], kp(w_in[l, :, 1664:1728]))
        cast(WPRE[:, :, 1472:1504], kp(w_in[l, :, 1696:1728]))
        cast(WPRE[:, :, 1504:1536], kp(w_in[l, :, 1664:1696]))
        wkv_src = self.w_kv_up[l].rearrange("(k p) (h e) -> p k h e", p=128, e=128)
        for k in range(2):
            cast(self.WKV[l][:, k, 0:512].rearrange("p (h e) -> p h e", e=64), wkv_src[:, k, :, 0:64])
            cast(self.WKV[l][:, k, 512:1024].rearrange("p (h e) -> p h e", e=64), wkv_src[:, k, :, 64:128])

    def _precast1(self, l, cast):
        kp = lambda ap: ap.rearrange("(k p) n -> p k n", p=128)
        w_in = self.w_in
        z = self.zeros
        cast(self.WQA[l][:, :, :], kp(w_in[l, :, 1024:1408]))
        cast(self.WQ[l][:, :, :], kp(self.w_q_up[l]))
        wq_src = self.w_q_up[l].rearrange("(k p) (h e) -> p k h e", p=128, e=128)
        for k in range(3):
            dst = self.WQR[l][:, k, :].rearrange("p (h e) -> p h e", e=128)
            cast(dst[:, :, 0:64], z[:, 0:512].rearrange("p (h e) -> p h e", e=64), reads=["zeros"])
            cast(dst[:, :, 64:96], wq_src[:, k, :, 96:128])
            cast(dst[:, :, 96:128], wq_src[:, k, :, 64:96])
        for c in range(8):
            cs = slice(c * 128, (c + 1) * 128)
            cast(self.WMG[l][c, :, 0:4, :], kp(self.w_conv_out[l, :, cs]))
            cast(self.WMG[l][c, :, 4:8, :], kp(self.w_attn_o[l, :, cs]))
            cast(self.WMG[l][c, :, 8:16, :], kp(w_in[l, :, 1728 + c * 128:1728 + (c + 1) * 128]))
            cast(self.WMG[l][c, :, 16:24, :], kp(w_in[l, :, 2752 + c * 128:2752 + (c + 1) * 128]))

    def _precast2(self, l, cast):
        kp = lambda ap: ap.rearrange("(k p) n -> p k n", p=128)
        cast(self.WOUT[l][:, :, :], kp(self.w_out[l]))

    def _precast3(self, l, cast):
        kp = lambda ap: ap.rearrange("(k p) n -> p k n", p=128)
        for g in range(4):
            cast(self.WUP[l][:, :, g * 1024:(g + 1) * 1024], kp(self.w_up[l, :, g * 1024:(g + 1) * 1024]))

    def _precast4(self, l, cast):
        kp = lambda ap: ap.rearrange("(k p) n -> p k n", p=128)
        for c in range(8):
            cs = slice(c * 128, (c + 1) * 128)
            cast(self.WDN[l][c, :, :, :], kp(self.w_dn[l, :, cs]))

    def wload(self, src, shape_str=None, grp=1, **kw):
        s = self.wslot()
        n = 1
        for d_ in src.shape[1:]:
            n *= d_
        dst = self.Wb[:, s, 0:n]
        if shape_str is not None:
            dst = dst.rearrange(shape_str, **kw)
        self.dma(self.sp, self.d_w[s], dst, src, writes=[("W", s)], extra=[self.pre_tok[self.l][grp]])
        return s, dst

    def rope_load(self, u, blk):
        i = self.rope_rr
        self.rope_rr ^= 1
        ti = 0 if u == 0 else 1
        cs = slice(blk * NBK, (blk + 1) * NBK)
        self.dma(self.sp, self.d_rope[i], self.RC[:, i, :], self.ropeC_d[ti, :, cs], writes=[("RC", i)])
        self.dma(self.sp, self.d_ropes[i], self.RS[:, i, :], self.ropeS_d[ti, :, cs], writes=[("RS", i)])
        return i

    def stat_begin(self):
        return self.bank_alloc()

    def stat_sq(self, bank, src_ap, src_keys, i, n):
        b = self.btile()
        self.op(self.act, lambda e: e.activation(out=self.Bt[:, b, :], in_=src_ap, func=AF.Square),
                reads=src_keys, writes=[("B", b)])
        self.mm_step(self.PS[bank][:, :], self.ones[:], self.Bt[:, b, :], i == 0, i == n - 1,
                     ["ones", ("B", b)], bank)

    def stat_rstd(self, bank, n):
        ps = self.PS[bank]
        f = self.rtile()
        ft = self.RT[:, f, :]
        self.op(self.act, lambda e: e.activation(out=ft, in_=ps[:, :], func=AF.Ln, scale=1.0 / n,
                                                 bias=self.epsb[:, 0:1]),
                reads=[("PS", bank), "epsb"], writes=[("RT", f)])
        self.bank_free(bank)
        self.op(self.act, lambda e: e.activation(out=ft, in_=ft, func=AF.Exp, scale=-0.5),
                reads=[("RT", f)], writes=[("RT", f)])
        return f

    def main_norm(self, blk, a_ap, b_ap):
        cs = slice(blk * NBK, (blk + 1) * NBK)
        bank = self.stat_begin()
        for c in range(8):
            self.stat_sq(bank, self.XT[:, c, cs], [("XT", c, blk)], c, 8)
        f = self.stat_rstd(bank, D)
        for c in range(8):
            t = self.ftile()
            self.op(self.dve, lambda e: e.tensor_tensor(out=self.Ft[:, t, :], in0=self.XT[:, c, cs],
                                                        in1=self.RT[:, f, :], op=ALU.mult),
                    reads=[("XT", c, blk), ("RT", f)], writes=[("F", t)])
            self.op(self.act, lambda e: e.activation(
                out=self.H[:, c, :], in_=self.Ft[:, t, :], func=AF.Identity, scale=a_ap[:, c:c + 1],
                bias=b_ap[:, c:c + 1]), reads=[("F", t), ("A1", self.l), ("A2", self.l), ("mod", self.l)], writes=[("H", c)])

    def unit(self, u):
        self.u = u
        self.rope_rr = 0
        if u == 0:
            for blk in range(self.NBLK):
                self.load_x_block(u, blk)
        if self.stop == 2:
            return
        for l in range(2):
            self.l = l
            self.pre_pass(u, l)
            if self.stop == 3:
                return
            for blk in range(self.NBLK):
                self.main_block(u, l, blk)
                if u == 0 and l == 0:
                    self.run_deferred(3 if blk == 0 else 2)
                if self.stop == 4:
                    return
            if u == 0 and l == 0:
                self.run_deferred(len(self.deferred))
            if self.stop == 5:
                break
        for blk in range(self.NBLK):
            self.final_out_block(u, blk)
            if u + 1 < self.NU and not self.stop:
                self.load_x_block(u + 1, blk)

    def stageq(self, i):
        return self.R32[:, 8192 + i * 2048:8192 + (i + 1) * 2048].bitcast(F32)

    def stage32(self, i):
        return self.R32[:, 12288 + i * 2048:12288 + (i + 1) * 2048].bitcast(F32)

    def load_x_block(self, u, blk0):
        for tt in range(blk0 * 4, blk0 * 4 + 4):
            i = tt % 2
            st = self.stageq(i)
            keys = [("QT", 4 * i + j) for j in range(4)]
            self.dma(self.sp, self.d_x[i], st, self.xs[u, tt * 128:(tt + 1) * 128, :], writes=keys)
            for half in range(2):
                bank = self.bank_alloc()
                ps = self.PS[bank]
                pe = self.pe
                for t in self.tr.deps(keys + ["ident"], [("PS", bank)]):
                    pe.wait(t)
                ins = None
                for j in range(4):
                    c = half * 4 + j
                    ins = pe.e.transpose(out=ps[:, j * 128:(j + 1) * 128], in_=st[:, c * 128:(c + 1) * 128],
                                         identity=self.ident[:])
                tok = pe.sig(ins)
                self.tr.record(tok, keys + ["ident"], [("PS", bank)])
                eng = self.act if half == 0 else self.dve
                blk = tt // 4
                wk = [("XT", half * 4 + j, blk) for j in range(4)]
                src = ps[:, :].rearrange("p (j n) -> p j n", n=128)
                dst = self.XT[:, half * 4:(half + 1) * 4, tt * 128:(tt + 1) * 128]
                if eng is self.act:
                    self.op(eng, lambda e: e.activation(out=dst, in_=src, func=AF.Copy),
                            reads=[("PS", bank)], writes=wk)
                else:
                    self.op(eng, lambda e: e.tensor_copy(out=dst, in_=src), reads=[("PS", bank)], writes=wk)
                self.bank_free(bank)

    def pre_pass(self, u, l):
        T, NBLK = self.T, self.NBLK
        act, dve, pool, sp = self.act, self.dve, self.pool, self.sp
        sample = (u == 0)
        allw = [("W", i) for i in range(3)] + [("KV", 3)]
        wpre = self.Wb[:, 0:3, :].rearrange("p s n -> p (s n)").rearrange("p (k n) -> p k n", k=8)
        wkv = self.R32[:, 6144:8192].rearrange("p (k n) -> p k n", k=2)
        self.dma(sp, self.d_w[0], wpre, self.WPRE[l][:, :, :], writes=allw, extra=[self.pre_tok[l][0]])
        self.dma(sp, self.d_w[0], wkv, self.WKV[l][:, :, :], writes=allw)
        self.w_rr = 0
        self.conv_build(l, 0)
        self.dg_prefetched = True
        self.wqa_pre = self.R32[:, 2048:5120].rearrange("p (k n) -> p k n", k=8)
        self.dma(sp, self.d_wqa, self.wqa_pre, self.WQA[l][:, :, :], writes=[("KV", 1), ("KV", 2)],
                 extra=[self.pre_tok[l][1]])
        self.wqa_prefetched = True
        self.op(pool, lambda e: e.memset(self.R32[:, 12288:16384], 1.0), writes=[("MIX", j) for j in range(8)])
        a_ap = self.A1[:, l, u, :]
        b_ap = self.mod[:, l, 0:8, u]
        GIv = self.GI[0:2048, :].rearrange("(h q p) t -> p h q t", q=2, p=128)
        for blk in range(NBLK):
            cs = slice(blk * NBK, (blk + 1) * NBK)
            ri = self.rope_load(u, blk)
            if blk == 0:
                self.main_norm(blk, a_ap, b_ap)
            hk = [("H", k) for k in range(8)]

            def glu_chunk(c):
                ba = self.bank_alloc()
                self.mm(self.PS[ba][:, :], [(wpre[:, k, c * 128:(c + 1) * 128], self.H[:, k, :]) for k in range(8)],
                        reads=allw + hk, bank=ba)
                bb = self.bank_alloc()
                self.mm(self.PS[bb][:, :], [(wpre[:, k, 512 + c * 128:512 + (c + 1) * 128], self.H[:, k, :])
                                            for k in range(8)], reads=allw + hk, bank=bb)
                f = self.ftile()
                self.op(act, lambda e: e.activation(out=self.Ft[:, f, :], in_=self.PS[bb][:, :], func=AF.Sigmoid),
                        reads=[("PS", bb)], writes=[("F", f)])
                self.bank_free(bb)
                self.op(dve, lambda e: e.tensor_tensor(
                    out=self.U[:, c, 16 + blk * NBK:16 + (blk + 1) * NBK], in0=self.PS[ba][:, :],
                    in1=self.Ft[:, f, :], op=ALU.mult), reads=[("PS", ba), ("F", f)], writes=[("U", c, blk)])
                self.bank_free(ba)

            kvf = []
            sbank = self.stat_begin()
            for c in range(2):
                bk = self.bank_alloc()
                self.mm(self.PS[bk][:, :], [(wpre[:, k, 1024 + c * 128:1024 + (c + 1) * 128], self.H[:, k, :])
                                            for k in range(8)], reads=allw + hk, bank=bk)
                f = self.ftile()
                self.op(act, lambda e: e.activation(out=self.Ft[:, f, :], in_=self.PS[bk][:, :], func=AF.Copy),
                        reads=[("PS", bk)], writes=[("F", f)])
                self.stat_sq(sbank, self.PS[bk][:, :], [("PS", bk)], c, 2)
                self.bank_free(bk)
                kvf.append(f)
            glu_chunk(0)
            glu_chunk(1)
            fr = self.stat_rstd(sbank, 256)
            for c in range(2):
                self.op(dve, lambda e: e.scalar_tensor_tensor(
                    out=self.KVA[:, c, :], in0=self.Ft[:, kvf[c], :], scalar=self.vec("gkv", l * 2 + c),
                    in1=self.RT[:, fr, :], op0=ALU.mult, op1=ALU.mult),
                    reads=[("F", kvf[c]), ("RT", fr), "vecs"], writes=[("KVA", c)])
            glu_chunk(2)
            glu_chunk(3)
            b1 = self.bank_alloc()
            self.mm(self.PS[b1][:, :], [(wpre[:, k, 1280:1408], self.H[:, k, :]) for k in range(8)],
                    reads=allw + hk, bank=b1)
            b2 = self.bank_alloc()
            self.mm(self.PS[b2][:, :], [(wpre[:, k, 1408:1536], self.H[:, k, :]) for k in range(8)],
                    reads=allw + hk, bank=b2)
            f1 = self.ftile()
            f2 = self.ftile()
            self.op(dve, lambda e: e.tensor_tensor(out=self.Ft[:, f1, :], in0=self.PS[b1][:, :],
                                                   in1=self.RC[:, ri, :], op=ALU.mult),
                    reads=[("PS", b1), ("RC", ri)], writes=[("F", f1)])
            self.bank_free(b1)
            self.op(dve, lambda e: e.tensor_tensor(out=self.Ft[:, f2, :], in0=self.PS[b2][:, :],
                                                   in1=self.RS[:, ri, :], op=ALU.mult),
                    reads=[("PS", b2), ("RS", ri)], writes=[("F", f2)])
            self.bank_free(b2)
            self.op(pool, lambda e: e.tensor_tensor(out=self.KRt, in0=self.Ft[:, f1, :], in1=self.Ft[:, f2, :],
                                                    op=ALU.add), reads=[("F", f1), ("F", f2)], writes=["KRt"])
            if blk + 1 < NBLK:
                self.main_norm(blk + 1, a_ap, b_ap)
            else:
                self.main_norm(0, a_ap, b_ap)
                self.norm_done = True
            for j in range(4):
                bk = self.bank_alloc()
                self.mm(self.PS[bk][:, :], [(wkv[:, k, j * 128:(j + 1) * 128], self.KVA[:, k, :]) for k in range(2)],
                        reads=allw + [("KVA", 0), ("KVA", 1)], bank=bk)
                self.op(act, lambda e, j=j, bk=bk: e.activation(out=self.KTs[0:64, 2 * j, :],
                                                               in_=self.PS[bk][0:64, :], func=AF.Copy),
                        reads=[("PS", bk)], writes=[("QT", 2 * j)])
                self.op(dve, lambda e, j=j, bk=bk: e.tensor_copy(out=self.KTs[0:64, 2 * j + 1, :],
                                                                in_=self.PS[bk][64:128, :]),
                        reads=[("PS", bk)], writes=[("QT", 2 * j + 1)])
                self.bank_free(bk)
            for tt in range(4):
                bk = self.bank_alloc()
                self.mm(self.PS[bk][:, :], [(self.KVA[:, k, tt * 128:(tt + 1) * 128], wkv[:, k, 512:1024])
                                            for k in range(2)], reads=allw + [("KVA", 0), ("KVA", 1)], bank=bk)
                src = self.PS[bk][:, :].rearrange("p (h e) -> p h e", e=64)
                mk = [("MIX", 2 * tt), ("MIX", 2 * tt + 1)]
                if tt % 2 == 0:
                    self.op(act, lambda e, tt=tt, src=src: e.activation(out=self.Vs[:, tt, :, 0:64], in_=src,
                                                                       func=AF.Copy),
                            reads=[("PS", bk)], writes=mk)
                else:
                    self.op(dve, lambda e, tt=tt, src=src: e.tensor_copy(out=self.Vs[:, tt, :, 0:64], in_=src),
                            reads=[("PS", bk)], writes=mk)
                self.bank_free(bk)
            qk = [("QT", h) for h in range(8)]
            self.dma(sp, self.d_st, GIv[0:64, :, 0, cs], self.KTs[0:64, :, :], reads=qk, writes=[("GI", "K", blk)])
            for h in range(8):
                self.dma(sp, self.d_st, GIv[64:128, h, 0, cs], self.KRt[64:128, :], reads=["KRt"],
                         writes=[("GI", "R", blk, h)])
            for tt in range(4):
                self.dma(sp, self.d_st, GIv[:, :, 1, (blk * 4 + tt) * 128:(blk * 4 + tt + 1) * 128],
                         self.Vs[:, tt, :, :], reads=[("MIX", 2 * tt), ("MIX", 2 * tt + 1)],
                         writes=[("GI", "V", blk, tt)])
        uall = [("U", c, b) for c in range(4) for b in range(NBLK)]
        if sample:
            self.op(pool, lambda e: e.tensor_copy(out=self.Hs[:, :, 0:16], in_=self.U[:, :, 16:32]),
                    reads=uall, writes=["Hs"])
            self.op(pool, lambda e: e.tensor_copy(out=self.Hs[:, :, 16:32], in_=self.U[:, :, T:T + 16]),
                    reads=uall, writes=["Hs"])
            halo_v = lambda ap: ap.rearrange("r (q x) -> (r q) x", x=128)
            HR = self.HR
            self.dma(sp, self.d_st, halo_v(self.GI[2048:2048 + HR, :]),
                     self.Hs[:, :, :].rearrange("p c j -> p (c j)"), reads=["Hs"], writes=[("GI", "H")])
            pool.wait(Tok(self.d_st.sem, self.d_st.count))
            rg = [[0, 1, 2, 3], [4, 5, 6, 7]]
            for h in ["H"] + list(range(8)):
                if h == "H":
                    src = self.GI[2048:2048 + HR, :]
                    dst = self.GO[8192:8192 + R * HR, :]
                else:
                    src = self.GI[h * 256:(h + 1) * 256, :]
                    dst = self.GO[h * 1024:(h + 1) * 1024, :]
                for t in self.tr.deps(self.gi_keys, [("GO", h)]):
                    pool.wait(t)
                ins = pool.e.collective_compute("AllGather", ALU.bypass, replica_groups=rg, ins=[src], outs=[dst])
                ci = 8 if h == "H" else h
                self.cc_count[ci] += 1
                ins.then_inc(self.cc_sem[ci], 1)
                self.tr.record(Tok(self.cc_sem[ci], self.cc_count[ci]), self.gi_keys, [("GO", h)])
            for r in range(R):
                base = 8192 + r * HR
                self.dma(sp, self.d_misc, self.HG[:, r, :, :].rearrange("p c j -> p (c j)"),
                         halo_v(self.GO[base:base + HR, :]), reads=[("GO", "H")], writes=["HG"])
            for side, (dst, src_sl, mname) in enumerate([(self.U[:, :, 0:16], slice(16, 32), "mL"),
                                                          (self.U[:, :, T + 16:T + 32], slice(0, 16), "mR")]):
                wk = [("U", c, "hl" if side == 0 else "hr") for c in range(4)]
                self.op(dve, lambda e: e.tensor_scalar(out=dst, in0=self.HG[:, 0, :, src_sl],
                                                       scalar1=self.vec(mname, 0), scalar2=None, op0=ALU.mult),
                        reads=["HG", "vecs"], writes=wk)
                for r in range(1, R):
                    self.op(dve, lambda e, r=r: e.scalar_tensor_tensor(
                        out=dst, in0=self.HG[:, r, :, src_sl], scalar=self.vec(mname, r), in1=dst,
                        op0=ALU.mult, op1=ALU.add), reads=["HG", "vecs"] + wk, writes=wk)
        else:
            self.op(pool, lambda e: e.memset(self.U[:, :, 0:16], 0.0), writes=[("U", c, "hl") for c in range(4)])
            self.op(pool, lambda e: e.memset(self.U[:, :, T + 16:T + 32], 0.0),
                    writes=[("U", c, "hr") for c in range(4)])

    def main_block(self, u, l, blk):
        T, NBLK = self.T, self.NBLK
        act, dve, pool, sp = self.act, self.dve, self.pool, self.sp
        sample = (u == 0)
        cs = slice(blk * NBK, (blk + 1) * NBK)
        vo = lambda name, per, c: self.vec(name, l * per + c)
        hk = [("H", k) for k in range(8)]
        if not self.norm_done:
            self.main_norm(blk, self.A1[:, l, u, :], self.mod[:, l, 0:8, u])
        self.norm_done = False
        self.conv_inline = not self.conv_done
        self.conv_done = False
        if self.conv_inline and not self.dg_prefetched:
            self.conv_build(l, 0)
        self.dg_prefetched = False
        if self.wqa_prefetched:
            wqa, wqk = self.wqa_pre, [("KV", 1), ("KV", 2)]
            self.wqa_prefetched = False
        else:
            s, wqa = self.wload(self.WQA[l][:, :, :], "p (k n) -> p k n", k=8)
            wqk = [("W", s)]
        qf = []
        sbank = self.stat_begin()
        for c in range(3):
            bk = self.bank_alloc()
            self.mm(self.PS[bk][:, :], [(wqa[:, k, c * 128:(c + 1) * 128], self.H[:, k, :]) for k in range(8)],
                    reads=wqk + hk, bank=bk)
            f = self.ftile()
            self.op(act, lambda e: e.activation(out=self.Ft[:, f, :], in_=self.PS[bk][:, :], func=AF.Copy),
                    reads=[("PS", bk)], writes=[("F", f)])
            self.stat_sq(sbank, self.PS[bk][:, :], [("PS", bk)], c, 3)
            self.bank_free(bk)
            qf.append(f)
        if self.conv_inline:
            self.conv_mm(l, blk, 0)
            self.conv_build(l, 1)
        fr = self.stat_rstd(sbank, 384)
        for c in range(3):
            self.op(dve, lambda e: e.scalar_tensor_tensor(
                out=self.QA[:, c, :], in0=self.Ft[:, qf[c], :], scalar=vo("gq", 3, c), in1=self.RT[:, fr, :],
                op0=ALU.mult, op1=ALU.mult), reads=[("F", qf[c]), ("RT", fr), "vecs"], writes=[("QA", c)])
        ri = self.rope_load(u, blk)
        if self.wq_prefetched:
            wq, wq_keys = self.wq_pre, ["DG"]
            self.wq_prefetched = False
        else:
            s1, wq = self.wload(self.WQ[l][:, :, :], "p (k n) -> p k n", k=3)
            wq_keys = [("W", s1)]
        s2, wqr = self.wload(self.WQR[l][:, :, :], "p (k n) -> p k n", k=3)
        qak = [("QA", c) for c in range(3)]
        for h in range(NH):
            b1 = self.bank_alloc()
            self.mm(self.PS[b1][:, :], [(wq[:, k, h * 128:(h + 1) * 128], self.QA[:, k, :]) for k in range(3)],
                    reads=wq_keys + qak, bank=b1)
            b2 = self.bank_alloc()
            self.mm(self.PS[b2][:, :], [(wqr[:, k, h * 128:(h + 1) * 128], self.QA[:, k, :]) for k in range(3)],
                    reads=[("W", s2)] + qak, bank=b2)
            f1, f2 = self.ftile(), self.ftile()
            self.op(dve, lambda e, f1=f1, b1=b1: e.tensor_tensor(out=self.Ft[:, f1, :], in0=self.PS[b1][:, :],
                                                                in1=self.RC[:, ri, :], op=ALU.mult),
                    reads=[("PS", b1), ("RC", ri)], writes=[("F", f1)])
            self.bank_free(b1)
            self.op(dve, lambda e, f2=f2, b2=b2: e.tensor_tensor(out=self.Ft[:, f2, :], in0=self.PS[b2][:, :],
                                                                in1=self.RS[:, ri, :], op=ALU.mult),
                    reads=[("PS", b2), ("RS", ri)], writes=[("F", f2)])
            self.bank_free(b2)
            self.op(pool, lambda e, h=h, f1=f1, f2=f2: e.tensor_tensor(
                out=self.QT[:, h, :], in0=self.Ft[:, f1, :], in1=self.Ft[:, f2, :], op=ALU.add),
                reads=[("F", f1), ("F", f2)], writes=[("QT", h)])
        self.attention(u, l, blk)
        zk = [("ZS", c) for c in range(4)]
        ak = [("A", c) for c in range(4)]
        for c in range(8):
            s, wm = self.wload(self.WMG[l][c, :, :, :], "p (k n) -> p k n", k=24)
            byc = self.bank_alloc()
            self.mm(self.PS[byc][:, :], [(wm[:, k, :], self.ZS[:, k, :]) for k in range(4)],
                    reads=[("W", s)] + zk, bank=byc)
            bya = self.bank_alloc()
            self.mm(self.PS[bya][:, :], [(wm[:, 4 + k, :], self.A[:, k, :]) for k in range(4)],
                    reads=[("W", s)] + ak, bank=bya)
            bgc = self.bank_alloc()
            self.mm(self.PS[bgc][:, :], [(wm[:, 8 + k, :], self.H[:, k, :]) for k in range(8)],
                    reads=[("W", s)] + hk, bank=bgc)
            bga = self.bank_alloc()
            self.mm(self.PS[bga][:, :], [(wm[:, 16 + k, :], self.H[:, k, :]) for k in range(8)],
                    reads=[("W", s)] + hk, bank=bga)
            f1, f2 = self.ftile(), self.ftile()
            self.op(act, lambda e, f1=f1, bgc=bgc: e.activation(out=self.Ft[:, f1, :], in_=self.PS[bgc][:, :],
                                                               func=AF.Sigmoid), reads=[("PS", bgc)], writes=[("F", f1)])
            self.bank_free(bgc)
            self.op(act, lambda e, f2=f2, bga=bga: e.activation(out=self.Ft[:, f2, :], in_=self.PS[bga][:, :],
                                                               func=AF.Sigmoid), reads=[("PS", bga)], writes=[("F", f2)])
            self.bank_free(bga)
            self.op(dve, lambda e, f1=f1, byc=byc: e.tensor_tensor(out=self.Ft[:, f1, :], in0=self.PS[byc][:, :],
                                                                  in1=self.Ft[:, f1, :], op=ALU.mult),
                    reads=[("PS", byc), ("F", f1)], writes=[("F", f1)])
            self.bank_free(byc)
            self.op(dve, lambda e, f2=f2, bya=bya: e.tensor_tensor(out=self.Ft[:, f2, :], in0=self.PS[bya][:, :],
                                                                  in1=self.Ft[:, f2, :], op=ALU.mult),
                    reads=[("PS", bya), ("F", f2)], writes=[("F", f2)])
            self.bank_free(bya)
            self.op(pool, lambda e, c=c, f1=f1, f2=f2: e.tensor_tensor(
                out=self.MIX[:, c, :], in0=self.Ft[:, f1, :], in1=self.Ft[:, f2, :], op=ALU.add),
                reads=[("F", f1), ("F", f2)], writes=[("MIX", c)])
        mk = [("MIX", c) for c in range(8)]
        for half in range(2):
            s, wo = self.wload(self.WOUT[l][:, :, half * 512:(half + 1) * 512], "p (k n) -> p k n", grp=2, k=8)
            for cc in range(4):
                c = half * 4 + cc
                bk = self.bank_alloc()
                self.mm(self.PS[bk][:, :], [(wo[:, k, cc * 128:(cc + 1) * 128], self.MIX[:, k, :]) for k in range(8)],
                        reads=[("W", s)] + mk, bank=bk)
                self.op(dve, lambda e, c=c, bk=bk: e.scalar_tensor_tensor(
                    out=self.XT[:, c, cs], in0=self.PS[bk][:, :], scalar=self.mod[:, l, 16 + c:17 + c, u],
                    in1=self.XT[:, c, cs], op0=ALU.mult, op1=ALU.add),
                    reads=[("PS", bk), ("mod", l), ("XT", c, blk)], writes=[("XT", c, blk)])
                self.bank_free(bk)
        self.main_norm(blk, self.A2[:, l, u, :], self.mod[:, l, 24:32, u])
        cops = self.conv_dve_ops(l, blk + 1) if blk + 1 < NBLK else []
        per = (len(cops) + 15) // 16

        def conv_slice():
            for _ in range(per):
                if cops:
                    cops.pop(0)()

        for g in range(8):
            s, wu = self.wload(self.WUP[l][:, :, g * 512:(g + 1) * 512], "p (k n) -> p k n", grp=3, k=8)
            for hc in range(4):
                i = g * 4 + hc
                bk = self.bank_alloc()
                self.mm(self.PS[bk][:, :], [(wu[:, k, hc * 128:(hc + 1) * 128], self.H[:, k, :]) for k in range(8)],
                        reads=[("W", s)] + hk, bank=bk)
                f = self.ftile()
                self.op(act, lambda e: e.activation(out=self.Ft[:, f, :], in_=self.PS[bk][:, :], func=AF.Relu),
                        reads=[("PS", bk)], writes=[("F", f)])
                self.bank_free(bk)
                self.op(pool, lambda e: e.tensor_tensor(out=self.HID[:, i, :], in0=self.Ft[:, f, :],
                                                        in1=self.Ft[:, f, :], op=ALU.mult),
                        reads=[("F", f)], writes=[self.hid_key(i)])
            conv_slice()
        if blk + 1 < NBLK:
            self.main_norm(blk + 1, self.A1[:, l, u, :], self.mod[:, l, 0:8, u])
            self.norm_done = True
        hidk = sorted(set(self.hid_key(i) for i in range(32)), key=str)
        for c in range(8):
            s, wd = self.wload(self.WDN[l][c, :, :, :], "p (k n) -> p k n", grp=4, k=32)
            bk = self.bank_alloc()
            self.mm(self.PS[bk][:, :], [(wd[:, k, :], self.HID[:, k, :]) for k in range(32)],
                    reads=[("W", s)] + hidk, bank=bk)
            self.op(dve, lambda e: e.scalar_tensor_tensor(
                out=self.XT[:, c, cs], in0=self.PS[bk][:, :], scalar=self.mod[:, l, 40 + c:41 + c, u],
                in1=self.XT[:, c, cs], op0=ALU.mult, op1=ALU.add),
                reads=[("PS", bk), ("mod", l), ("XT", c, blk)], writes=[("XT", c, blk)])
            self.bank_free(bk)
            conv_slice()
        if blk + 1 < NBLK:
            while cops:
                cops.pop(0)()
            self.conv_ln(l)
            self.conv_done = True

    @staticmethod
    def hid_key(i):
        if i < 16:
            return ("KV", i // 4)
        if i < 24:
            return ("QT", i - 16)
        return ("MIX", i - 24)

    def conv_build(self, l, c):
        self.dma(self.sp, self.d_dgl, self.DG[:, 0:3968], self.WDG[l][c, :, :], writes=["DG"], extra=[self.dg_tok])

    def conv_mm(self, l, blk, c):
        NBLK = self.NBLK
        rk = [("U", c, blk), ("U", c, blk - 1) if blk > 0 else ("U", c, "hl"),
              ("U", c, blk + 1) if blk < NBLK - 1 else ("U", c, "hr")]
        bk = self.bank_alloc()
        pairs = [(self.DG[:, k * 128:(k + 1) * 128], self.U[:, c, blk * NBK + 1 + k:blk * NBK + 1 + k + NBK])
                 for k in range(31)]
        self.mm(self.PS[bk][:, :], pairs, reads=["DG"] + rk, bank=bk)
        self.op(self.act, lambda e: e.activation(out=self.ZS[:, c, :], in_=self.PS[bk][:, :], func=AF.Identity,
                                                 bias=self.vec("cb", l * 4 + c)),
                reads=[("PS", bk), "vecs"], writes=[("ZS", c)])
        self.bank_free(bk)

    def conv_ln(self, l):
        act, dve, pool = self.act, self.dve, self.pool
        bm = self.bank_alloc()
        bq = self.bank_alloc()
        for c in range(4):
            self.mm_step(self.PS[bm][:, :], self.ones[:], self.ZS[:, c, :], c == 0, c == 3, ["ones", ("ZS", c)], bm)
            self.stat_sq(bq, self.ZS[:, c, :], [("ZS", c)], c, 4)
        fm, fv = self.rtile(), self.rtile()
        self.op(act, lambda e: e.activation(out=self.RT[:, fm, :], in_=self.PS[bm][:, :], func=AF.Copy,
                                            scale=1.0 / 512), reads=[("PS", bm)], writes=[("RT", fm)])
        self.bank_free(bm)
        self.op(dve, lambda e: e.tensor_tensor(out=self.RT[:, fv, :], in0=self.RT[:, fm, :], in1=self.RT[:, fm, :],
                                               op=ALU.mult), reads=[("RT", fm)], writes=[("RT", fv)])
        self.op(dve, lambda e: e.scalar_tensor_tensor(out=self.RT[:, fv, :], in0=self.PS[bq][:, :], scalar=1.0 / 512,
                                                      in1=self.RT[:, fv, :], op0=ALU.mult, op1=ALU.subtract),
                reads=[("PS", bq), ("RT", fv)], writes=[("RT", fv)])
        self.bank_free(bq)
        self.op(act, lambda e: e.activation(out=self.RT[:, fv, :], in_=self.RT[:, fv, :], func=AF.Ln,
                                            bias=self.epsb[:, 0:1]), reads=[("RT", fv), "epsb"], writes=[("RT", fv)])
        self.op(act, lambda e: e.activation(out=self.RT[:, fv, :], in_=self.RT[:, fv, :], func=AF.Exp, scale=-0.5),
                reads=[("RT", fv)], writes=[("RT", fv)])
        for c in range(4):
            f = self.ftile()
            self.op(dve, lambda e: e.tensor_tensor(out=self.Ft[:, f, :], in0=self.ZS[:, c, :], in1=self.RT[:, fm, :],
                                                   op=ALU.subtract), reads=[("ZS", c), ("RT", fm)], writes=[("F", f)])
            self.op(pool, lambda e: e.tensor_tensor(out=self.Ft[:, f, :], in0=self.Ft[:, f, :], in1=self.RT[:, fv, :],
                                                    op=ALU.mult), reads=[("F", f), ("RT", fv)], writes=[("F", f)])
            self.op(act, lambda e: e.activation(out=self.ZS[:, c, :], in_=self.Ft[:, f, :], func=AF.Silu,
                                                scale=self.vec("lg", l * 4 + c), bias=self.vec("lb", l * 4 + c)),
                    reads=[("F", f), "vecs"], writes=[("ZS", c)])

    def conv_dve_ops(self, l, blk):
        NBLK = self.NBLK
        accA = self.PT[:, 0:2, :].rearrange("p a n -> p (a n)").bitcast(F32)
        accB = self.QA[:, 0:2, :].rearrange("p a n -> p (a n)").bitcast(F32)
        ka = [("PT", 0), ("PT", 1)]
        kb = [("QA", 0), ("QA", 1)]
        ops = []
        for c in range(4):
            rk = [("U", c, blk), ("U", c, blk - 1) if blk > 0 else ("U", c, "hl"),
                  ("U", c, blk + 1) if blk < NBLK - 1 else ("U", c, "hr"), "vecs"]
            for k in range(31):
                src = self.U[:, c, blk * NBK + 1 + k:blk * NBK + 1 + k + NBK]
                wcol = self.vec("cw", (l * 4 + c) * 31 + k)
                if k == 0:
                    ops.append(lambda src=src, wcol=wcol, c=c, rk=rk: self.op(self.dve, lambda e: e.tensor_scalar(
                        out=accA, in0=src, scalar1=wcol, scalar2=self.vec("cb", l * 4 + c), op0=ALU.mult,
                        op1=ALU.add), reads=rk, writes=ka))
                elif k == 1:
                    ops.append(lambda src=src, wcol=wcol, rk=rk: self.op(self.dve, lambda e: e.tensor_scalar(
                        out=accB, in0=src, scalar1=wcol, scalar2=None, op0=ALU.mult), reads=rk, writes=kb))
                else:
                    acc, kk = (accA, ka) if k % 2 == 0 else (accB, kb)
                    ops.append(lambda src=src, wcol=wcol, rk=rk, acc=acc, kk=kk: self.op(
                        self.dve, lambda e: e.scalar_tensor_tensor(out=acc, in0=src, scalar=wcol, in1=acc,
                                                                   op0=ALU.mult, op1=ALU.add),
                        reads=rk + kk, writes=kk))
            ops.append(lambda c=c: self.op(self.pool, lambda e: e.tensor_tensor(
                out=self.ZS[:, c, :], in0=accA, in1=accB, op=ALU.add), reads=ka + kb, writes=[("ZS", c)]))
        return ops

    def att_hook(self, l, blk, h):
        if not self.conv_inline:
            return
        if h < 3:
            self.conv_mm(l, blk, h + 1)
            if h < 2:
                self.conv_build(l, h + 2)
        elif h == 3:
            self.conv_ln(l)

    def attention(self, u, l, blk):
        act, dve, sp, pe = self.act, self.dve, self.sp, self.pe
        sample = (u == 0)
        KP = self.KP
        nkc = KP // 128
        if sample:
            pieces = [(r, p) for r in range(R) for p in range(self.T // KP)]
        else:
            pieces = [(None, p) for p in range(self.T // KP)]
        steps = []
        for h in range(NH):
            for pi, (r, p) in enumerate(pieces):
                for kc in range(nkc):
                    steps.append(dict(h=h, pi=pi, r=r, p=p, kc=kc, first=(pi == 0 and kc == 0),
                                      last=(pi == len(pieces) - 1 and kc == nkc - 1)))
        pbuf = {}
        accb = {}

        def issue_loads(st):
            h = st["h"]
            if st["kc"] == 0:
                i = self.kv_rr
                self.kv_rr ^= 1
                pbuf[(h, st["pi"])] = i
                r, p = st["r"], st["p"]
                if r is None:
                    base, ro, rkeys = self.GI, h * 256, self.gi_keys
                else:
                    base, ro, rkeys = self.GO, h * 1024 + r * 256, [("GO", h)]
                ksrc = base[ro:ro + 128, p * KP:(p + 1) * KP]
                vsrc = base[ro + 128:ro + 256, p * KP:(p + 1) * KP]
                self.dma(sp, self.d_kv[i], self.Kb[i][:, 0:KP], ksrc, reads=rkeys, writes=[("KV", 2 * i)])
                self.dma(sp, self.d_vv[i], self.Vb[i][:, 0:nkc, :].rearrange("p k e -> p (k e)"), vsrc,
                         reads=rkeys, writes=[("KV", 2 * i + 1)])
            st["buf"] = pbuf[(h, st["pi"])]

        def emit_Spair(sa, sb2):
            issue_loads(sa)
            issue_loads(sb2)
            b0 = self.pair_alloc()
            for st, bk in ((sa, b0), (sb2, b0 + 1)):
                i, kc, h = st["buf"], st["kc"], st["h"]
                self.mm(self.PS[bk][:, :], [(self.Kb[i][:, kc * 128:(kc + 1) * 128], self.QT[:, h, :])],
                        reads=[("KV", 2 * i), ("QT", h)], bank=bk)
            q = self.ptp_rr
            self.ptp_rr ^= 1
            sa["pt"], sb2["pt"] = 2 * q, 2 * q + 1
            self.op(act, lambda e: e.activation(out=self.PT[:, 2 * q:2 * q + 2, :], in_=self.PSall[:, b0:b0 + 2, :],
                                                func=AF.Exp, scale=ATT_SCALE),
                    reads=[("PS", b0), ("PS", b0 + 1)], writes=[("PT", 2 * q), ("PT", 2 * q + 1)])
            self.bank_free(b0)
            self.bank_free(b0 + 1)

        def emit_PV(st):
            h, i, kc, pt = st["h"], st["buf"], st["kc"], st["pt"]
            if st["first"]:
                accb[h] = self.bank_alloc()
            acc = accb[h]
            accp = self.PS[acc]
            self.mm_step(accp[:, :], self.Vb[i][:, kc, :], self.PT[:, pt, :], st["first"], st["last"],
                         [("KV", 2 * i + 1), ("PT", pt)], acc)
            if st["last"]:
                f = self.ftile()
                self.op(dve, lambda e: e.reciprocal(out=self.Ft[0:64, f, :], in_=accp[64:128, :]),
                        reads=[("PS", acc)], writes=[("F", f)])
                po = (h % 2) * 64
                self.op(dve, lambda e: e.tensor_tensor(out=self.A[po:po + 64, h // 2, :], in0=accp[0:64, :],
                                                       in1=self.Ft[0:64, f, :], op=ALU.mult),
                        reads=[("PS", acc), ("F", f)], writes=[("A", h // 2)])
                self.bank_free(acc)
                self.att_hook(l, blk, h)

        def emit_S(st):
            issue_loads(st)
            i, kc, h = st["buf"], st["kc"], st["h"]
            sb_ = self.bank_alloc()
            self.mm(self.PS[sb_][:, :], [(self.Kb[i][:, kc * 128:(kc + 1) * 128], self.QT[:, h, :])],
                    reads=[("KV", 2 * i), ("QT", h)], bank=sb_)
            pt = self.pt_rr
            self.pt_rr = (self.pt_rr + 1) % 4
            st["pt"] = pt
            self.op(act, lambda e: e.activation(out=self.PT[:, pt, :], in_=self.PS[sb_][:, :], func=AF.Exp,
                                                scale=ATT_SCALE), reads=[("PS", sb_)], writes=[("PT", pt)])
            self.bank_free(sb_)

        LA = 3
        n = len(steps)
        for j in range(n + LA):
            if j < n:
                emit_S(steps[j])
            if j >= LA:
                emit_PV(steps[j - LA])

    def final_out_block(self, u, blk):
        T, NBLK = self.T, self.NBLK
        act, dve, pool, pe = self.act, self.dve, self.pool, self.pe
        YT = self.R32[:, 0:8192].bitcast(F32).rearrange("p (c n) -> p c n", n=NBK)
        if True:
            cs = slice(blk * NBK, (blk + 1) * NBK)
            sbank = self.stat_begin()
            for c in range(8):
                self.stat_sq(sbank, self.XT[:, c, cs], [("XT", c, blk)], c, 8)
            f = self.stat_rstd(sbank, D)
            for c in range(8):
                self.op(dve, lambda e, c=c: e.scalar_tensor_tensor(
                    out=YT[:, c, :], in0=self.XT[:, c, cs], scalar=self.vec("fg", c), in1=self.RT[:, f, :],
                    op0=ALU.mult, op1=ALU.mult), reads=[("XT", c, blk), ("RT", f), "vecs"], writes=[("KV", c // 2)])
            ytk = [("KV", j) for j in range(4)]
            for t4 in range(4):
                tt = blk * 4 + t4
                i = self.out_rr
                self.out_rr ^= 1
                st = self.stage32(i)
                keys = [("MIX", 4 * i + j) for j in range(4)]
                for half in range(2):
                    bank = self.bank_alloc()
                    ps = self.PS[bank]
                    for t in self.tr.deps(ytk + ["ident"], [("PS", bank)]):
                        pe.wait(t)
                    ins = None
                    for j in range(4):
                        c = half * 4 + j
                        ins = pe.e.transpose(out=ps[:, j * 128:(j + 1) * 128], in_=YT[:, c, t4 * 128:(t4 + 1) * 128],
                                             identity=self.ident[:])
                    tok = pe.sig(ins)
                    self.tr.record(tok, ytk + ["ident"], [("PS", bank)])
                    dst = st[:, half * 512:(half + 1) * 512]
                    hk_ = keys[half * 2:half * 2 + 2]
                    if half == 0:
                        self.op(act, lambda e, dst=dst, ps=ps: e.activation(out=dst, in_=ps[:, :], func=AF.Copy),
                                reads=[("PS", bank)], writes=hk_)
                    else:
                        self.op(dve, lambda e, dst=dst, ps=ps: e.tensor_copy(out=dst, in_=ps[:, :]),
                                reads=[("PS", bank)], writes=hk_)
                    self.bank_free(bank)
                self.dma(pool, self.d_out[i], self.ys[u, tt * 128:(tt + 1) * 128, :], st, reads=keys)


def _make_program(T, NU):
    p = Prog(T, NU)
    p.kv_rr = 0
    p.pt_rr = 0
    p.out_rr = 0
    p.rope_rr = 0
    p.l = 0
    p.norm_done = False
    p.dg_prefetched = False
    p.wqa_prefetched = False
    p.wq_prefetched = False
    p.conv_done = False
    p.conv_inline = True
    return p.build()


def _rope_tables(pos0, T):
    inv = (np.float32(10000.0) ** (-np.arange(0, 64, 2, dtype=np.float32) / np.float32(64))).astype(np.float32)
    ang = (np.arange(pos0, pos0 + T, dtype=np.float32)[:, None] * inv[None, :]).astype(np.float32)
    cos = np.cos(ang).astype(np.float32).T
    sin = np.sin(ang).astype(np.float32).T
    C = np.ones((128, T), np.float32)
    S = np.zeros((128, T), np.float32)
    C[64:96] = cos
    C[96:128] = cos
    S[64:96] = -sin
    S[96:128] = sin
    return C, S


def _pm(v, n):
    v = np.asarray(v, np.float32)
    lead = v.shape[:-1]
    return np.moveaxis(v.reshape(lead + (n, 128)), -1, 0)


def run(inputs, T, NPC):
    NU = NPC + 1
    f32 = lambda a: np.ascontiguousarray(np.asarray(a, dtype=np.float32))
    xp, xsm = f32(inputs["x_prompt"]), f32(inputs["x_sample"])
    cp, csm = f32(inputs["c_prompt"]), f32(inputs["c_sample"])
    nc = _make_program(T, NU)
    vec_common = np.zeros((128, NV), np.float32)

    def put(name, arr):
        arr = np.asarray(arr, np.float32).reshape(128, -1)
        vec_common[:, _VOFF[name]:_VOFF[name] + arr.shape[1]] = arr

    put("ada_b", _pm(inputs["ada_b"], 48))
    put("g1", _pm(inputs["norm_mix_g"], 8))
    put("g2", _pm(inputs["norm_mlp_g"], 8))
    put("fg", _pm(inputs["final_g"], 8))
    put("gq", _pm(inputs["q_norm_g"], 3))
    put("gkv", _pm(inputs["kv_norm_g"], 2))
    put("cb", _pm(inputs["conv_dw_b"], 4))
    put("lg", _pm(inputs["conv_ln_g"], 4))
    put("lb", _pm(inputs["conv_ln_b"], 4))
    cw = np.asarray(inputs["conv_dw"], np.float32)
    cw = cw.reshape(2, 31, 4, 128).transpose(3, 0, 2, 1)
    put("cw", cw)
    Cp, Sp = _rope_tables(0, T)
    shared = {k: f32(inputs[k]) for k in ["ada_w", "w_in", "w_q_up", "w_kv_up", "w_attn_o", "w_conv_out",
                                          "w_out", "w_mlp_up", "w_mlp_down"]}
    in_maps = []
    for core in range(NCORES):
        s, q = core // R, core % R
        xs = np.empty((NU, T, D), np.float32)
        xs[0] = xsm[s, q * T:(q + 1) * T]
        xs[1:] = xp[core * NPC:(core + 1) * NPC]
        c = np.concatenate([csm[s:s + 1], cp[core * NPC:(core + 1) * NPC]], axis=0)
        cT = np.ascontiguousarray(c.reshape(NU, 8, 128).transpose(2, 1, 0))
        vecs = vec_common.copy()
        if q > 0:
            vecs[:, _VOFF["mL"] + q - 1] = 1.0
        if q < R - 1:
            vecs[:, _VOFF["mR"] + q + 1] = 1.0
        Cs, Ss = _rope_tables(q * T, T)
        m = {"xs": xs, "cT": cT, "vecs": vecs,
             "ropeC": np.ascontiguousarray(np.stack([Cs, Cp])), "ropeS": np.ascontiguousarray(np.stack([Ss, Sp]))}
        m.update(shared)
        in_maps.append(m)
    res = run_bass_kernel_spmd(nc, in_maps, core_ids=list(range(NCORES)))
    y_prompt = np.empty_like(xp)
    y_sample = np.empty_like(xsm)
    for core in range(NCORES):
        s, q = core // R, core % R
        ys = res.results[core]["ys"]
        y_sample[s, q * T:(q + 1) * T] = ys[0]
        y_prompt[core * NPC:(core + 1) * NPC] = ys[1:]
    return y_prompt, y_sample


def kernel(**inputs):
    return run(inputs, 2048, 4)
```
